# Optimizing a Trainium2 kernel written in Bass

```python
import math
import jax
import jax.numpy as jnp
from jax import lax
import numpy as np

D_MODEL = 1024
BATCH = 4
SEQ = 4096
DEPTH = 4

N_MIXERS = 4
GROUP_W = D_MODEL // N_MIXERS
HEAD_DIM = 64
N_HEADS = GROUP_W // HEAD_DIM
S5_GROUP_CH = 16
S5_GROUPS = GROUP_W // S5_GROUP_CH
S5_STATE = 64
RET_CHUNK = 128
ROPE_BASE = 10000.0
RWKV_LORA_W = 64
RWKV_LORA_A = 64
RWKV_LORA_V = 32
RWKV_LORA_G = 128
RWKV_GN_EPS = 64e-5
SB_BLOCK = 128
D_FF = 4 * D_MODEL
NORM_EPS = 1e-6

OFF_S5 = 0
OFF_RET = OFF_S5 + GROUP_W
OFF_SB = OFF_RET + 4 * GROUP_W
OFF_RW = OFF_SB + 3 * GROUP_W
N_RW0 = 3 * GROUP_W + RWKV_LORA_W + RWKV_LORA_A + RWKV_LORA_G
N_IN0 = OFF_RW + N_RW0
N_IN = N_IN0 + RWKV_LORA_V

kernel_name = "hybrid_s5_retnet_rwkv7_stickbreak_trunk"


def rmsnorm(x, gain):
    xf = x.astype(jnp.float32)
    y = xf * lax.rsqrt(jnp.mean(xf * xf, axis=-1, keepdims=True) + NORM_EPS)
    return (y * gain.astype(jnp.float32)).astype(x.dtype)


def token_shift(t):
    return jnp.pad(t[:, :-1], ((0, 0), (1, 0), (0, 0)))


def rope_tables(seq):
    inv_freq = ROPE_BASE ** (-jnp.arange(0, HEAD_DIM, 2, dtype=jnp.float32) / HEAD_DIM)
    ang = jnp.arange(seq, dtype=jnp.float32)[:, None] * inv_freq[None, :]
    return jnp.cos(ang), jnp.sin(ang)


def apply_rope(t, cos, sin):
    t1, t2 = jnp.split(t.astype(jnp.float32), 2, axis=-1)
    c = cos[None, :, None, :]
    s = sin[None, :, None, :]
    return jnp.concatenate([t1 * c - t2 * s, t1 * s + t2 * c], axis=-1)


def _complex_affine_combine(e1, e2):
    a1r, a1i, b1r, b1i = e1
    a2r, a2i, b2r, b2i = e2
    return (a2r * a1r - a2i * a1i,
            a2r * a1i + a2i * a1r,
            a2r * b1r - a2i * b1i + b2r,
            a2r * b1i + a2i * b1r + b2i)


def s5_mixer(u, a_re, a_im, log_dt, b_re, b_im, c_re, c_im, d_skip, glu_w1, glu_w2):
    f32 = jnp.float32
    bsz, seq, _ = u.shape
    uf = u.astype(f32).reshape(bsz, seq, S5_GROUPS, S5_GROUP_CH)
    a_re = a_re.astype(f32)
    a_im = a_im.astype(f32)
    dt = jnp.exp(log_dt.astype(f32))[:, None]
    mag = jnp.exp(a_re * dt)
    ab_re = mag * jnp.cos(a_im * dt)
    ab_im = mag * jnp.sin(a_im * dt)
    den = a_re * a_re + a_im * a_im
    num_re = ab_re - 1.0
    zoh_re = (num_re * a_re + ab_im * a_im) / den
    zoh_im = (ab_im * a_re - num_re * a_im) / den
    b_re = b_re.astype(f32)
    b_im = b_im.astype(f32)
    bb_re = zoh_re[..., None] * b_re - zoh_im[..., None] * b_im
    bb_im = zoh_re[..., None] * b_im + zoh_im[..., None] * b_re
    bu_re = jnp.einsum("bsgc,gpc->bsgp", uf, bb_re)
    bu_im = jnp.einsum("bsgc,gpc->bsgp", uf, bb_im)
    a_re_t = jnp.broadcast_to(ab_re, bu_re.shape)
    a_im_t = jnp.broadcast_to(ab_im, bu_re.shape)
    _, _, x_re, x_im = lax.associative_scan(
        _complex_affine_combine, (a_re_t, a_im_t, bu_re, bu_im), axis=1)
    y = (jnp.einsum("bsgp,gcp->bsgc", x_re, c_re.astype(f32))
         - jnp.einsum("bsgp,gcp->bsgc", x_im, c_im.astype(f32))
         + d_skip.astype(f32).reshape(S5_GROUPS, S5_GROUP_CH) * uf)
    y = jax.nn.gelu(y.reshape(bsz, seq, GROUP_W)).astype(u.dtype)
    return ((y @ glu_w1) * jax.nn.sigmoid(y @ glu_w2)).astype(u.dtype)


def retention_mixer(q, k, v, g, cos, sin):
    f32 = jnp.float32
    bsz, seq, _ = q.shape
    n_chunks = seq // RET_CHUNK
    q = apply_rope(q.reshape(bsz, seq, N_HEADS, HEAD_DIM), cos, sin)
    k = apply_rope(k.reshape(bsz, seq, N_HEADS, HEAD_DIM), cos, sin) * HEAD_DIM ** -0.5
    v = v.reshape(bsz, seq, N_HEADS, HEAD_DIM).astype(f32)
    log_gamma = jnp.log1p(-jnp.exp2(-5.0 - jnp.arange(N_HEADS, dtype=f32)))
    idx = jnp.arange(RET_CHUNK, dtype=f32)
    rel = idx[:, None] - idx[None, :]
    intra = jnp.where(rel >= 0, jnp.exp(log_gamma[:, None, None] * jnp.maximum(rel, 0.0)), 0.0)
    qc = q.reshape(bsz, n_chunks, RET_CHUNK, N_HEADS, HEAD_DIM)
    kc = k.reshape(bsz, n_chunks, RET_CHUNK, N_HEADS, HEAD_DIM)
    vc = v.reshape(bsz, n_chunks, RET_CHUNK, N_HEADS, HEAD_DIM)
    scores = jnp.einsum("bnihd,bnjhd->bnhij", qc, kc) * intra
    o_intra = jnp.einsum("bnhij,bnjhd->bnihd", scores, vc)
    k_decay = jnp.exp(log_gamma[None, :] * (RET_CHUNK - 1.0 - idx)[:, None])
    chunk_kv = jnp.einsum("bnjhd,jh,bnjhe->nbhde", kc, k_decay, vc)
    chunk_decay = jnp.exp(log_gamma * RET_CHUNK)[None, :, None, None]

    def step(state, kv):
        return chunk_decay * state + kv, state

    _, prev_states = lax.scan(
        step, jnp.zeros((bsz, N_HEADS, HEAD_DIM, HEAD_DIM), f32), chunk_kv)
    q_decay = jnp.exp(log_gamma[None, :] * (idx + 1.0)[:, None])
    o_cross = jnp.einsum("bnihd,ih,nbhde->bnihe", qc, q_decay, prev_states)
    o = (o_intra + o_cross).reshape(bsz, seq, N_HEADS, HEAD_DIM)
    o = o * lax.rsqrt(jnp.mean(o * o, axis=-1, keepdims=True) + NORM_EPS)
    return (jax.nn.silu(g.astype(f32)) * o.reshape(bsz, seq, GROUP_W)).astype(g.dtype)


def rwkv7_mixer(cols, vres_cols, v_first, mu, vres_mu, w0, w2, a0, a2, g2,
                v0, v2, k_k, k_a, r_k, ln_w, ln_b):
    f32 = jnp.float32
    bsz, seq, _ = cols.shape
    xs = cols + (token_shift(cols) - cols) * mu
    r, k, v, wd, ad, gd = jnp.split(
        xs, [GROUP_W, 2 * GROUP_W, 3 * GROUP_W, 3 * GROUP_W + RWKV_LORA_W,
             3 * GROUP_W + RWKV_LORA_W + RWKV_LORA_A], axis=-1)
    w_log = -jax.nn.softplus(-(w0 + jnp.tanh(wd) @ w2)) - 0.5
    decay = jnp.exp(-jnp.exp(w_log.astype(f32)))
    a = jax.nn.sigmoid(a0 + ad @ a2)
    g = jax.nn.sigmoid(gd) @ g2
    if v_first is None:
        v_first = v
    else:
        vx = vres_cols + (token_shift(vres_cols) - vres_cols) * vres_mu
        v = v + (v_first - v) * jax.nn.sigmoid(v0 + vx @ v2)

    def heads(t):
        return t.astype(f32).reshape(bsz, seq, N_HEADS, HEAD_DIM)

    kk = heads(k * k_k)
    kk = kk * lax.rsqrt(jnp.maximum(jnp.sum(kk * kk, axis=-1, keepdims=True), 1e-12))
    k = k * (1.0 + (a - 1.0) * k_a)
    r_h, k_h, v_h, a_h, w_h = heads(r), heads(k), heads(v), heads(a), heads(decay)

    def step(state, inp):
        r_t, w_t, k_t, v_t, aa_t, bb_t = inp
        sa = jnp.einsum("bhvk,bhk->bhv", state, aa_t)
        state = (state * w_t[:, :, None, :] + sa[..., None] * bb_t[:, :, None, :]
                 + v_t[..., None] * k_t[:, :, None, :])
        return state, jnp.einsum("bhvk,bhk->bhv", state, r_t)

    scan_in = tuple(jnp.moveaxis(t, 1, 0) for t in (r_h, w_h, k_h, v_h, -kk, kk * a_h))
    _, ys = lax.scan(step, jnp.zeros((bsz, N_HEADS, HEAD_DIM, HEAD_DIM), f32), scan_in)
    y = jnp.moveaxis(ys, 0, 1)
    mean = jnp.mean(y, axis=-1, keepdims=True)
    var = jnp.mean(jnp.square(y - mean), axis=-1, keepdims=True)
    y = ((y - mean) * lax.rsqrt(var + RWKV_GN_EPS)).reshape(bsz, seq, GROUP_W)
    y = y * ln_w.astype(f32) + ln_b.astype(f32)
    bonus = jnp.sum(r_h * k_h * r_k.astype(f32).reshape(N_HEADS, HEAD_DIM), axis=-1, keepdims=True) * v_h
    out = (y + bonus.reshape(bsz, seq, GROUP_W)) * g.astype(f32)
    return out.astype(cols.dtype), v_first


def stick_breaking_mixer(q, k, v):
    f32 = jnp.float32
    bsz, seq, _ = q.shape
    q = q.reshape(bsz, seq, N_HEADS, HEAD_DIM) * HEAD_DIM ** -0.5
    k = k.reshape(bsz, seq, N_HEADS, HEAD_DIM)
    v = v.reshape(bsz, seq, N_HEADS, HEAD_DIM)
    key_pos = jnp.arange(seq)

    def block(i):
        start = i * SB_BLOCK
        qb = lax.dynamic_slice_in_dim(q, start, SB_BLOCK, axis=1)
        z = jnp.einsum("bqhd,bkhd->bhqk", qb, k).astype(f32)
        q_pos = start + jnp.arange(SB_BLOCK)
        causal = key_pos[None, :] < q_pos[:, None]
        log_keep = jnp.where(causal, -jax.nn.softplus(z), 0.0)
        later = lax.cumsum(log_keep, axis=3, reverse=True) - log_keep
        w = jnp.where(causal, jnp.exp(jax.nn.log_sigmoid(z) + later), 0.0)
        return jnp.einsum("bhqk,bkhd->bqhd", w, v.astype(f32))

    out = lax.map(block, jnp.arange(seq // SB_BLOCK))
    out = jnp.moveaxis(out, 0, 1).reshape(bsz, seq, GROUP_W)
    return out.astype(q.dtype)


def setup_inputs(seed: int = 0) -> dict:
    key = jax.random.key(seed)
    ks = jax.random.split(key, 34)
    f32 = jnp.float32

    def nrm(k, shape, scale):
        return jax.random.normal(k, shape, f32) * scale

    def gain(k, shape):
        return 1.0 + 0.02 * jax.random.normal(k, shape, f32)

    G, P, C = S5_GROUPS, S5_STATE, S5_GROUP_CH
    n_frac = jnp.arange(GROUP_W, dtype=f32) / (GROUP_W - 1)
    w0_profile = -7.0 + 5.0 * n_frac ** 0.85 + 0.5
    return {
        "x": nrm(ks[0], (BATCH, SEQ, D_MODEL), 1.0),
        "norm_mix_pre": gain(ks[1], (DEPTH, D_MODEL)),
        "norm_mix_post": gain(ks[2], (DEPTH, D_MODEL)),
        "norm_ffn_pre": gain(ks[3], (DEPTH, D_MODEL)),
        "norm_ffn_post": gain(ks[4], (DEPTH, D_MODEL)),
        "w_in_first": nrm(ks[5], (D_MODEL, N_IN0), D_MODEL ** -0.5),
        "w_in_rest": nrm(ks[6], (DEPTH - 1, D_MODEL, N_IN), D_MODEL ** -0.5),
        "w_out": nrm(ks[7], (DEPTH, D_MODEL, D_MODEL), D_MODEL ** -0.5),
        "s5_a_re": -0.5 + nrm(ks[8], (DEPTH, G, P), 0.01),
        "s5_a_im": math.pi * jnp.arange(P, dtype=f32) + nrm(ks[9], (DEPTH, G, P), 0.01),
        "s5_log_dt": jax.random.uniform(ks[10], (DEPTH, G), f32, math.log(1e-3), math.log(1e-1)),
        "s5_b_re": nrm(ks[11], (DEPTH, G, P, C), (2 * C) ** -0.5),
        "s5_b_im": nrm(ks[12], (DEPTH, G, P, C), (2 * C) ** -0.5),
        "s5_c_re": nrm(ks[13], (DEPTH, G, C, P), (2 * P) ** -0.5),
        "s5_c_im": nrm(ks[14], (DEPTH, G, C, P), (2 * P) ** -0.5),
        "s5_d": nrm(ks[15], (DEPTH, GROUP_W), 1.0),
        "s5_glu_w1": nrm(ks[16], (DEPTH, GROUP_W, GROUP_W), GROUP_W ** -0.5),
        "s5_glu_w2": nrm(ks[17], (DEPTH, GROUP_W, GROUP_W), GROUP_W ** -0.5),
        "rw_mu": jax.random.uniform(ks[18], (DEPTH, N_RW0), f32),
        "rw_vres_mu": jax.random.uniform(ks[19], (DEPTH - 1, RWKV_LORA_V), f32),
        "rw_w0": w0_profile + nrm(ks[20], (DEPTH, GROUP_W), 0.1),
        "rw_w2": nrm(ks[21], (DEPTH, RWKV_LORA_W, GROUP_W), 0.5 * RWKV_LORA_W ** -0.5),
        "rw_a0": nrm(ks[22], (DEPTH, GROUP_W), 0.1),
        "rw_a2": nrm(ks[23], (DEPTH, RWKV_LORA_A, GROUP_W), RWKV_LORA_A ** -0.5),
        "rw_g2": nrm(ks[24], (DEPTH, RWKV_LORA_G, GROUP_W), RWKV_LORA_G ** -0.5),
        "rw_v0": nrm(ks[25], (DEPTH - 1, GROUP_W), 0.1),
        "rw_v2": nrm(ks[26], (DEPTH - 1, RWKV_LORA_V, GROUP_W), RWKV_LORA_V ** -0.5),
        "rw_k_k": 0.85 + nrm(ks[27], (DEPTH, GROUP_W), 0.02),
        "rw_k_a": 1.0 + nrm(ks[28], (DEPTH, GROUP_W), 0.02),
        "rw_r_k": nrm(ks[29], (DEPTH, GROUP_W), 0.1),
        "rw_ln_w": gain(ks[30], (DEPTH, GROUP_W)),
        "rw_ln_b": nrm(ks[31], (DEPTH, GROUP_W), 0.02),
        "w_up": nrm(ks[32], (DEPTH, D_MODEL, D_FF), D_MODEL ** -0.5),
        "w_down": nrm(ks[33], (DEPTH, D_FF, D_MODEL), D_FF ** -0.5),
    }


def reference(x, norm_mix_pre, norm_mix_post, norm_ffn_pre, norm_ffn_post,
              w_in_first, w_in_rest, w_out,
              s5_a_re, s5_a_im, s5_log_dt, s5_b_re, s5_b_im, s5_c_re, s5_c_im,
              s5_d, s5_glu_w1, s5_glu_w2,
              rw_mu, rw_vres_mu, rw_w0, rw_w2, rw_a0, rw_a2, rw_g2, rw_v0, rw_v2,
              rw_k_k, rw_k_a, rw_r_k, rw_ln_w, rw_ln_b,
              w_up, w_down):
    seq = x.shape[1]
    cos, sin = rope_tables(seq)
    v_first = None
    for l in range(DEPTH):
        h = rmsnorm(x, norm_mix_pre[l])
        w_in = w_in_first if l == 0 else w_in_rest[l - 1]
        proj = h @ w_in
        s5_u = proj[..., OFF_S5:OFF_RET]
        ret_q, ret_k, ret_v, ret_g = jnp.split(proj[..., OFF_RET:OFF_SB], 4, axis=-1)
        sb_q, sb_k, sb_v = jnp.split(proj[..., OFF_SB:OFF_RW], 3, axis=-1)
        rw_cols = proj[..., OFF_RW:N_IN0]
        out_s5 = s5_mixer(s5_u, s5_a_re[l], s5_a_im[l], s5_log_dt[l], s5_b_re[l], s5_b_im[l],
                          s5_c_re[l], s5_c_im[l], s5_d[l], s5_glu_w1[l], s5_glu_w2[l])
        out_ret = retention_mixer(ret_q, ret_k, ret_v, ret_g, cos, sin)
        if l == 0:
            out_rw, v_first = rwkv7_mixer(
                rw_cols, None, None, rw_mu[l], None, rw_w0[l], rw_w2[l], rw_a0[l], rw_a2[l],
                rw_g2[l], None, None, rw_k_k[l], rw_k_a[l], rw_r_k[l], rw_ln_w[l], rw_ln_b[l])
        else:
            out_rw, v_first = rwkv7_mixer(
                rw_cols, proj[..., N_IN0:], v_first, rw_mu[l], rw_vres_mu[l - 1], rw_w0[l],
                rw_w2[l], rw_a0[l], rw_a2[l], rw_g2[l], rw_v0[l - 1], rw_v2[l - 1],
                rw_k_k[l], rw_k_a[l], rw_r_k[l], rw_ln_w[l], rw_ln_b[l])
        out_sb = stick_breaking_mixer(sb_q, sb_k, sb_v)
        mixed = jnp.concatenate([out_s5, out_ret, out_rw, out_sb], axis=-1) @ w_out[l]
        x = x + rmsnorm(mixed, norm_mix_post[l])
        h = rmsnorm(x, norm_ffn_pre[l])
        f = jnp.square(jax.nn.relu(h @ w_up[l])) @ w_down[l]
        x = x + rmsnorm(f, norm_ffn_post[l])
    return x
```

```python
import contextlib
import numpy as np
import ml_dtypes
import concourse.bass as bass
import concourse.mybir as mybir
from concourse.bass_utils import run_bass_kernel_spmd

F32 = mybir.dt.float32
BF16 = mybir.dt.bfloat16
AF = mybir.ActivationFunctionType
ALU = mybir.AluOpType
NPBF = ml_dtypes.bfloat16

D = 1024
SEQ = 4096
BATCH = 4
DEPTH = 4
NCORE = 8
TOK = 2048
N_IN = 3104
N_IN0 = 3072
DFF = 4096
EPS = 1e-6


class Res:
    __slots__ = ("w", "rs")

    def __init__(self):
        self.w = None
        self.rs = {}


class FW:
    ENGS = ("pe", "act", "dve", "pool", "sp")

    _SEMSTACKS = {}

    def __init__(self, nc, stack, n_dma=10):
        stack = FW._SEMSTACKS.setdefault(id(nc), contextlib.ExitStack())
        self.nc = nc
        self.eng = dict(pe=nc.tensor, act=nc.scalar, dve=nc.vector, pool=nc.gpsimd, sp=nc.sync)
        self.sem = {e: stack.enter_context(nc.semaphore(_PFX[0] + "s_" + e)) for e in self.ENGS}
        self.cnt = {e: 0 for e in self.ENGS}
        self.known = {e: {} for e in self.ENGS}
        self.dsem = [stack.enter_context(nc.semaphore(_PFX[0] + "d%d" % i)) for i in range(n_dma)]
        self.dcnt = [0] * n_dma
        self.dnext = 0
        self.out_stamps = []
        self.nwait = 0

    def need(self, eng, stamp):
        if stamp is None:
            return
        kind, key, val = stamp
        if kind == "e" and key == "pe" and eng == "pe":
            return
        k = (kind, key)
        if self.known[eng].get(k, 0) >= val:
            return
        sem = self.sem[key] if kind == "e" else self.dsem[key]
        self.eng[eng].wait_ge(sem, val)
        self.nwait += 1
        self.known[eng][k] = val

    def _pre(self, eng, reads, writes):
        for r in reads:
            self.need(eng, r.w)
        for w in writes:
            self.need(eng, w.w)
            for s in list(w.rs.values()):
                self.need(eng, s)

    def _post(self, stamp, reads, writes):
        for r in reads:
            r.rs[(stamp[0], stamp[1])] = stamp
        for w in writes:
            w.w = stamp
            w.rs = {}

    def op(self, eng, fn, reads=(), writes=()):
        self._pre(eng, reads, writes)
        ins = fn(self.eng[eng])
        self.cnt[eng] += 1
        ins.then_inc(self.sem[eng], 1)
        stamp = ("e", eng, self.cnt[eng])
        self._post(stamp, reads, writes)
        return stamp

    def dma(self, q, out, in_, reads=(), writes=(), is_output=False):
        self._pre(q, reads, writes)
        i = self.dnext
        self.dnext = (self.dnext + 1) % len(self.dsem)
        if self.dcnt[i] > 0:
            self.need(q, ("d", i, self.dcnt[i]))
        ins = self.eng[q].dma_start(out=out, in_=in_)
        self.dcnt[i] += 16
        ins.then_inc(self.dsem[i], 16)
        stamp = ("d", i, self.dcnt[i])
        self._post(stamp, reads, writes)
        if is_output:
            self.out_stamps.append(stamp)
        return stamp

    def finish(self):
        for e in self.ENGS:
            for x in self.ENGS:
                if x != e and self.cnt[x]:
                    self.need(e, ("e", x, self.cnt[x]))
            for i, c in enumerate(self.dcnt):
                if c:
                    self.need(e, ("d", i, c))


_PFX = [""]


def _sb(nc, stack, name, shape, dt):
    return stack.enter_context(nc.sbuf_tensor(_PFX[0] + name, shape, dt))


def _ps(nc, stack, name, shape, dt=F32):
    return stack.enter_context(nc.psum_tensor(_PFX[0] + name, shape, dt))


def _dram(nc, name, shape, dt, kind):
    return nc.dram_tensor(_PFX[0] + name, shape, dt, kind=kind)


def _newnc():
    return bass.Bass("TRN2", target_bir_lowering=False)


NF = 3616
NT = 1024


def build_p1(nc=None, ext=None):
    nc = nc or _newnc()
    ext = {} if ext is None else ext
    xT = ext["xT"] if "xT" in ext else _dram(nc, "xT", [D, TOK], F32, kind="ExternalInput").ap()
    w_in = _dram(nc, "w_in", [D, NF], F32, kind="ExternalInput").ap()
    w_tok = _dram(nc, "w_tok", [D, NT], F32, kind="ExternalInput").ap()
    gain = _dram(nc, "gain", [128, 8], F32, kind="ExternalInput").ap()
    projT = _dram(nc, "projT", [NF, TOK], BF16, kind="ExternalOutput").ap()
    projK = _dram(nc, "projK", [TOK, NT], BF16, kind="ExternalOutput").ap()
    NB = TOK // 512
    with contextlib.ExitStack() as st:
        fw = FW(nc, st)
        xt = _sb(nc, st, "xt", [128, 8, TOK], F32)
        ht = _sb(nc, st, "ht", [128, 8, TOK], BF16)
        w = _sb(nc, st, "w", [128, 8, NF], BF16)
        wt = _sb(nc, st, "wt", [128, 8, NT], BF16)
        g = _sb(nc, st, "g", [128, 8], F32)
        ones = _sb(nc, st, "ones", [128, 128], F32)
        epsb = _sb(nc, st, "epsb", [128, 1], F32)
        rstd = _sb(nc, st, "rstd", [128, 512], F32)
        sqp = Rot([_sb(nc, st, "sq%d" % i, [128, 512], F32) for i in range(2)])
        obp = Rot([_sb(nc, st, "ob%d" % i, [128, 512], BF16) for i in range(4)])
        psp = Rot([_ps(nc, st, "ps%d" % i, [128, 512]) for i in range(8)])
        r_xt = [Res() for _ in range(NB)]
        r_ht = [Res() for _ in range(NB)]
        r_w = [Res() for _ in range(8)]
        r_wt = Res()
        r_g, r_ones, r_rstd, r_eps = Res(), Res(), Res(), Res()
        fw.op("pool", lambda e: e.memset(ones[:, :], 1.0), writes=[r_ones])
        fw.op("pool", lambda e: e.memset(epsb[:, :], EPS), writes=[r_eps])
        fw.dma("sp", g[:, :], gain, writes=[r_g])
        xTv = xT.rearrange("(k p) t -> p k t", p=128)
        for b in range(NB):
            fw.dma("sp", xt[:, :, b * 512:(b + 1) * 512], xTv[:, :, b * 512:(b + 1) * 512], writes=[r_xt[b]])
        w_v = w_in.rearrange("(k p) n -> p k n", p=128)
        for k in range(8):
            fw.dma("pool", w[:, k, :], w_v[:, k, :], writes=[r_w[k]])
        fw.dma("pool", wt[:, :, :], w_tok.rearrange("(k p) n -> p k n", p=128), writes=[r_wt])
        for b in range(NB):
            sl = slice(b * 512, (b + 1) * 512)
            emit_norm_rstd(fw, lambda k: xt[:, k, sl], r_xt[b], 8, 512, ones, r_ones, epsb, r_eps, sqp, psp, rstd, r_rstd)
            for k in range(8):
                fw.op("dve", lambda e, k=k: e.scalar_tensor_tensor(out=ht[:, k, sl], in0=xt[:, k, sl],
                                                                   scalar=g[:, k:k + 1], in1=rstd[:, :],
                                                                   op0=ALU.mult, op1=ALU.mult),
                      reads=[r_xt[b], r_g, r_rstd], writes=[r_ht[b]])
        oi = 0
        ntile = (NF + 127) // 128
        for b in range(NB):
            sl = slice(b * 512, (b + 1) * 512)
            for n in range(ntile):
                c0 = n * 128
                cw = min(128, NF - c0)
                pst, rps = psp.next()
                for k in range(8):
                    fw.op("pe", lambda e, k=k, pst=pst: e.matmul(pst[:cw, :], lhsT=w[:, k, c0:c0 + cw], rhs=ht[:, k, sl],
                                                                 start=(k == 0), stop=(k == 7)),
                          reads=[r_w[k], r_ht[b]], writes=[rps])
                o, ro = obp.next()
                if oi % 2 == 0:
                    fw.op("act", lambda e, o=o, pst=pst: e.copy(out=o[:cw, :], in_=pst[:cw, :]), reads=[rps], writes=[ro])
                else:
                    fw.op("dve", lambda e, o=o, pst=pst: e.tensor_copy(out=o[:cw, :], in_=pst[:cw, :]), reads=[rps], writes=[ro])
                fw.dma("sp", projT[c0:c0 + cw, sl], o[:cw, :], reads=[ro], is_output=True)
                oi += 1
            for tt in range(4):
                ts_ = slice(b * 512 + tt * 128, b * 512 + (tt + 1) * 128)
                for ch in range(NT // 512):
                    pst, rps = psp.next()
                    for k in range(8):
                        fw.op("pe", lambda e, k=k, pst=pst: e.matmul(pst[:, :], lhsT=ht[:, k, ts_], rhs=wt[:, k, ch * 512:(ch + 1) * 512],
                                                                     start=(k == 0), stop=(k == 7)),
                              reads=[r_wt, r_ht[b]], writes=[rps])
                    o, ro = obp.next()
                    if oi % 2 == 0:
                        fw.op("act", lambda e, o=o, pst=pst: e.copy(out=o[:, :], in_=pst[:, :]), reads=[rps], writes=[ro])
                    else:
                        fw.op("dve", lambda e, o=o, pst=pst: e.tensor_copy(out=o[:, :], in_=pst[:, :]), reads=[rps], writes=[ro])
                    fw.dma("sp", projK[ts_, ch * 512:(ch + 1) * 512], o[:, :], reads=[ro], is_output=True)
                    oi += 1
        fw.finish()
    return nc


_CACHE = {}


def _get(name, fn):
    if name not in _CACHE:
        _CACHE[name] = fn()
    return _CACHE[name]


def gain_layout(gvec):
    return np.ascontiguousarray(gvec.reshape(8, 128).T)


class Rot:
    def __init__(self, bufs):
        self.bufs = bufs
        self.res = [Res() for _ in bufs]
        self.i = 0

    def next(self):
        j = self.i % len(self.bufs)
        self.i += 1
        return self.bufs[j], self.res[j]


def emit_norm_rstd(fw, src, r_src, nk, n, ones, r_ones, epsb, r_eps, sqp, psp, rstd, r_rstd):
    pst, rps = psp.next()
    for k in range(nk):
        s, rs = sqp.next()
        fw.op("act", lambda e, k=k, s=s: e.activation(out=s[:, :n], in_=src(k), func=AF.Square),
              reads=[r_src], writes=[rs])
        fw.op("pe", lambda e, k=k, s=s: e.matmul(pst[:, :n], lhsT=ones[:, :], rhs=s[:, :n],
                                                 start=(k == 0), stop=(k == nk - 1)),
              reads=[rs, r_ones], writes=[rps])
    fw.op("act", lambda e: e.activation(out=rstd[:, :n], in_=pst[:, :n], func=AF.Ln, bias=epsb[:, 0:1],
                                        scale=1.0 / (nk * 128)), reads=[rps, r_eps], writes=[r_rstd])
    fw.op("act", lambda e: e.activation(out=rstd[:, :n], in_=rstd[:, :n], func=AF.Exp, scale=-0.5),
          reads=[r_rstd], writes=[r_rstd])


def build_p3(nc=None, ext=None):
    nc = nc or _newnc()
    ext = {} if ext is None else ext
    xT = _dram(nc, "xT", [D, TOK], F32, kind="ExternalInput").ap()
    mixT = _dram(nc, "mixT", [D, TOK], BF16, kind="ExternalInput").ap()
    glu1 = _dram(nc, "glu1", [256, 256], F32, kind="ExternalInput").ap()
    glu2 = _dram(nc, "glu2", [256, 256], F32, kind="ExternalInput").ap()
    w_out = _dram(nc, "w_out", [D, D], F32, kind="ExternalInput").ap()
    w_up = _dram(nc, "w_up", [D, DFF], F32, kind="ExternalInput").ap()
    w_down = _dram(nc, "w_down", [DFF, D], F32, kind="ExternalInput").ap()
    gains = _dram(nc, "gains", [128, 24], F32, kind="ExternalInput").ap()
    outT = _dram(nc, "outT", [D, TOK], F32, kind="ExternalOutput").ap()
    ext["_outT"] = outT
    H = 1024
    with contextlib.ExitStack() as st:
        fw = FW(nc, st)
        xt = _sb(nc, st, "xt", [128, 8, H], F32)
        mx = _sb(nc, st, "mx", [128, 8, H], BF16)
        aT = _sb(nc, st, "aT", [128, 32, H], BF16)
        m = _sb(nc, st, "m", [128, 8, H], F32)
        h2 = mx
        wo = _sb(nc, st, "wo", [128, 8, D], BF16)
        g1 = _sb(nc, st, "g1", [128, 2, 256], BF16)
        g2 = _sb(nc, st, "g2", [128, 2, 256], BF16)
        gn = _sb(nc, st, "gn", [128, 24], F32)
        ones = _sb(nc, st, "ones", [128, 128], F32)
        epsb = _sb(nc, st, "epsb", [128, 1], F32)
        rstd = _sb(nc, st, "rstd", [128, 512], F32)
        tmp = Rot([_sb(nc, st, "tmp%d" % i, [128, 512], F32) for i in range(3)])
        sqp = Rot([_sb(nc, st, "sq%d" % i, [128, 512], F32) for i in range(2)])
        wch = Rot([_sb(nc, st, "wch%d" % i, [128, 32 * 128], BF16) for i in range(3)])
        psp = Rot([_ps(nc, st, "ps%d" % i, [128, 512]) for i in range(8)])
        r_xt = [Res() for _ in range(2)]
        r_mx = [Res() for _ in range(2)]
        r_m = [Res() for _ in range(2)]
        r_h2 = r_mx
        r_aT = [[Res() for _ in range(2)] for _ in range(32)]
        r_wo, r_g1, r_g2, r_gn, r_ones, r_eps, r_rstd = (Res() for _ in range(7))

        fw.op("pool", lambda e: e.memset(ones[:, :], 1.0), writes=[r_ones])
        fw.op("pool", lambda e: e.memset(epsb[:, :], EPS), writes=[r_eps])
        fw.dma("sp", gn[:, :], gains, writes=[r_gn])
        fw.dma("pool", wo[:, :, :], w_out.rearrange("(k p) n -> p k n", p=128), writes=[r_wo])
        fw.dma("pool", g1[:, :, :], glu1.rearrange("(k p) n -> p k n", p=128), writes=[r_g1])
        fw.dma("pool", g2[:, :, :], glu2.rearrange("(k p) n -> p k n", p=128), writes=[r_g2])
        xTv = xT.rearrange("(k p) t -> p k t", p=128)
        mTv = mixT.rearrange("(k p) t -> p k t", p=128)
        oTv = outT.rearrange("(k p) t -> p k t", p=128)
        wuv = w_up.rearrange("(k p) f -> p k f", p=128)
        wdv = w_down.rearrange("(f p) d -> p f d", p=128)

        for half in range(2):
            t0 = half * H
            for b in range(2):
                sl = slice(b * 512, (b + 1) * 512)
                gsl = slice(t0 + b * 512, t0 + (b + 1) * 512)
                fw.dma("sp", xt[:, :, sl], xTv[:, :, gsl], writes=[r_xt[b]])
                fw.dma("sp", mx[:, :, sl], mTv[:, :, gsl], writes=[r_mx[b]])
            for b in range(2):
                sl = slice(b * 512, (b + 1) * 512)
                outs = []
                for n in range(2):
                    p1, rp1 = psp.next()
                    p2, rp2 = psp.next()
                    for k in range(2):
                        fw.op("pe", lambda e, k=k, n=n, p1=p1: e.matmul(p1[:, :], lhsT=g1[:, k, n * 128:(n + 1) * 128],
                                                                        rhs=mx[:, k, sl], start=(k == 0), stop=(k == 1)),
                              reads=[r_g1, r_mx[b]], writes=[rp1])
                    for k in range(2):
                        fw.op("pe", lambda e, k=k, n=n, p2=p2: e.matmul(p2[:, :], lhsT=g2[:, k, n * 128:(n + 1) * 128],
                                                                        rhs=mx[:, k, sl], start=(k == 0), stop=(k == 1)),
                              reads=[r_g2, r_mx[b]], writes=[rp2])
                    t, rt = tmp.next()
                    fw.op("act", lambda e, t=t, p2=p2: e.activation(out=t[:, :], in_=p2[:, :], func=AF.Sigmoid),
                          reads=[rp2], writes=[rt])
                    outs.append((n, p1, rp1, t, rt))
                for (n, p1, rp1, t, rt) in outs:
                    fw.op("dve", lambda e, n=n, p1=p1, t=t: e.tensor_tensor(out=mx[:, n, sl], in0=p1[:, :], in1=t[:, :],
                                                                           op=ALU.mult),
                          reads=[rp1, rt], writes=[r_mx[b]])
            for b in range(2):
                sl = slice(b * 512, (b + 1) * 512)
                for n in range(8):
                    p, rp = psp.next()
                    for k in range(8):
                        fw.op("pe", lambda e, k=k, n=n, p=p: e.matmul(p[:, :], lhsT=wo[:, k, n * 128:(n + 1) * 128],
                                                                      rhs=mx[:, k, sl], start=(k == 0), stop=(k == 7)),
                              reads=[r_wo, r_mx[b]], writes=[rp])
                    if n % 2 == 0:
                        fw.op("act", lambda e, n=n, p=p: e.copy(out=m[:, n, sl], in_=p[:, :]), reads=[rp], writes=[r_m[b]])
                    else:
                        fw.op("dve", lambda e, n=n, p=p: e.tensor_copy(out=m[:, n, sl], in_=p[:, :]), reads=[rp],
                              writes=[r_m[b]])
                emit_norm_rstd(fw, lambda k: m[:, k, sl], r_m[b], 8, 512, ones, r_ones, epsb, r_eps, sqp, psp, rstd, r_rstd)
                for k in range(8):
                    t, rt = tmp.next()
                    fw.op("dve", lambda e, k=k, t=t: e.scalar_tensor_tensor(out=t[:, :], in0=m[:, k, sl], scalar=gn[:, k:k + 1],
                                                                           in1=rstd[:, :], op0=ALU.mult, op1=ALU.mult),
                          reads=[r_m[b], r_gn, r_rstd], writes=[rt])
                    fw.op("pool", lambda e, k=k, t=t: e.tensor_tensor(out=xt[:, k, sl], in0=xt[:, k, sl], in1=t[:, :], op=ALU.add),
                          reads=[rt, r_xt[b]], writes=[r_xt[b]])
                emit_norm_rstd(fw, lambda k: xt[:, k, sl], r_xt[b], 8, 512, ones, r_ones, epsb, r_eps, sqp, psp, rstd, r_rstd)
                for k in range(8):
                    fw.op("dve", lambda e, k=k: e.scalar_tensor_tensor(out=h2[:, k, sl], in0=xt[:, k, sl],
                                                                       scalar=gn[:, 8 + k:9 + k], in1=rstd[:, :],
                                                                       op0=ALU.mult, op1=ALU.mult),
                          reads=[r_xt[b], r_gn, r_rstd], writes=[r_h2[b]])
            for c in range(8):
                wb, rw = wch.next()
                wv = wb[:, :].rearrange("p (k f) -> p k f", k=8)
                fw.dma("pool", wv, wuv[:, :, c * 512:(c + 1) * 512], writes=[rw])
                for fi in range(4):
                    f = c * 4 + fi
                    for b in range(2):
                        sl = slice(b * 512, (b + 1) * 512)
                        p, rp = psp.next()
                        for k in range(8):
                            fw.op("pe", lambda e, k=k, fi=fi, p=p, wv=wv: e.matmul(p[:, :], lhsT=wv[:, k, fi * 128:(fi + 1) * 128],
                                                                                 rhs=h2[:, k, sl], start=(k == 0), stop=(k == 7)),
                                  reads=[rw, r_h2[b]], writes=[rp])
                        t, rt = tmp.next()
                        fw.op("act", lambda e, t=t, p=p: e.activation(out=t[:, :], in_=p[:, :], func=AF.Relu),
                              reads=[rp], writes=[rt])
                        eng = "dve" if (f + b) % 2 == 0 else "pool"
                        fw.op(eng, lambda e, t=t, f=f: e.tensor_tensor(out=aT[:, f, sl], in0=t[:, :], in1=t[:, :], op=ALU.mult),
                              reads=[rt], writes=[r_aT[f][b]])
            for n in range(8):
                wb, rw = wch.next()
                wv = wb[:, :].rearrange("p (f d) -> p f d", f=32)
                fw.dma("pool", wv, wdv[:, :, n * 128:(n + 1) * 128], writes=[rw])
                for b in range(2):
                    sl = slice(b * 512, (b + 1) * 512)
                    p, rp = psp.next()
                    for f in range(32):
                        fw.op("pe", lambda e, f=f, p=p, wv=wv: e.matmul(p[:, :], lhsT=wv[:, f, :], rhs=aT[:, f, sl],
                                                                        start=(f == 0), stop=(f == 31)),
                              reads=[rw, r_aT[f][b]], writes=[rp])
                    if n % 2 == 0:
                        fw.op("act", lambda e, n=n, p=p: e.copy(out=m[:, n, sl], in_=p[:, :]), reads=[rp], writes=[r_m[b]])
                    else:
                        fw.op("dve", lambda e, n=n, p=p: e.tensor_copy(out=m[:, n, sl], in_=p[:, :]), reads=[rp],
                              writes=[r_m[b]])
            for b in range(2):
                sl = slice(b * 512, (b + 1) * 512)
                gsl = slice(t0 + b * 512, t0 + (b + 1) * 512)
                emit_norm_rstd(fw, lambda k: m[:, k, sl], r_m[b], 8, 512, ones, r_ones, epsb, r_eps, sqp, psp, rstd, r_rstd)
                for k in range(8):
                    t, rt = tmp.next()
                    fw.op("dve", lambda e, k=k, t=t: e.scalar_tensor_tensor(out=t[:, :], in0=m[:, k, sl],
                                                                           scalar=gn[:, 16 + k:17 + k], in1=rstd[:, :],
                                                                           op0=ALU.mult, op1=ALU.mult),
                          reads=[r_m[b], r_gn, r_rstd], writes=[rt])
                    fw.op("pool", lambda e, k=k, t=t: e.tensor_tensor(out=xt[:, k, sl], in0=xt[:, k, sl], in1=t[:, :], op=ALU.add),
                          reads=[rt, r_xt[b]], writes=[r_xt[b]])
                fw.dma("sp", oTv[:, :, gsl], xt[:, :, sl], reads=[r_xt[b]], is_output=True)
        fw.finish()
    return nc


def run_p3(xT_list, mixT_list, glu1, glu2, w_out, w_up, w_down, g_post, g_fpre, g_fpost):
    nc = _get("p3", build_p3)
    gl = np.ascontiguousarray(np.concatenate([gain_layout(g_post), gain_layout(g_fpre), gain_layout(g_fpost)], axis=1))
    com = {"glu1": np.ascontiguousarray(glu1), "glu2": np.ascontiguousarray(glu2), "w_out": np.ascontiguousarray(w_out),
           "w_up": np.ascontiguousarray(w_up), "w_down": np.ascontiguousarray(w_down), "gains": gl}
    in_maps = [dict(com, xT=xT_list[c], mixT=mixT_list[c]) for c in range(NCORE)]
    res = run_bass_kernel_spmd(nc, in_maps, core_ids=list(range(NCORE)))
    return [r["outT"] for r in res.results]


def sb_consts():
    k = np.arange(128)[:, None]
    q = np.arange(512)[None, :]
    m = np.stack([(k + 128 * i < q) for i in range(4)], axis=1).astype(np.float32)
    j = np.arange(128)[:, None]
    kk = np.arange(128)[None, :]
    ntri = -(j > kk).astype(np.float32)
    return {"mask": np.ascontiguousarray(m.reshape(128, 2048)), "ntri": np.ascontiguousarray(ntri)}


def build_sb(nc=None, ext=None):
    nc = nc or _newnc()
    ext = {} if ext is None else ext
    qT = _dram(nc, "qT", [128, SEQ], BF16, kind="ExternalInput").ap()
    kT = _dram(nc, "kT", [128, SEQ], BF16, kind="ExternalInput").ap()
    vtok = _dram(nc, "vtok", [SEQ, 128], BF16, kind="ExternalInput").ap()
    maskd = _dram(nc, "mask", [128, 2048], F32, kind="ExternalInput").ap()
    ntrid = _dram(nc, "ntri", [128, 128], F32, kind="ExternalInput").ap()
    oT = _dram(nc, "oT", [128, SEQ], BF16, kind="ExternalOutput").ap()
    with contextlib.ExitStack() as st:
        fw = FW(nc, st)
        q_sb = _sb(nc, st, "q_sb", [128, SEQ], BF16)
        k_sb = _sb(nc, st, "k_sb", [128, SEQ], BF16)
        v_sb = _sb(nc, st, "v_sb", [128, 32, 128], BF16)
        mask = _sb(nc, st, "mask_sb", [128, 4, 512], F32)
        ntri = _sb(nc, st, "ntri_sb", [128, 128], F32)
        nones = _sb(nc, st, "nones", [128, 128], F32)
        oneb = _sb(nc, st, "oneb", [128, 1], F32)
        o_sb = [_sb(nc, st, "o_sb%d" % h, [64, SEQ], BF16) for h in range(2)]
        e_p = Rot([_sb(nc, st, "e%d" % i, [128, 512], F32) for i in range(2)])
        sp_p = Rot([_sb(nc, st, "sp%d" % i, [128, 512], F32) for i in range(3)])
        t_p = Rot([_sb(nc, st, "t%d" % i, [128, 512], F32) for i in range(2)])
        w_p = Rot([_sb(nc, st, "w%d" % i, [128, 512], BF16) for i in range(3)])
        sum_p = Rot([_sb(nc, st, "sum%d" % i, [128, 512], F32) for i in range(2)])
        psA = Rot([_ps(nc, st, "psA%d" % i, [128, 512]) for i in range(3)])
        psB = Rot([_ps(nc, st, "psB%d" % i, [128, 512]) for i in range(3)])
        psO = Rot([_ps(nc, st, "psO%d" % i, [128, 512]) for i in range(2)])
        r_q, r_k, r_v, r_mask, r_ntri, r_nones, r_one = (Res() for _ in range(7))
        r_o = [Res(), Res()]
        fw.op("pool", lambda e: e.memset(nones[:, :], -1.0), writes=[r_nones])
        fw.op("pool", lambda e: e.memset(oneb[:, :], 1.0), writes=[r_one])
        fw.dma("sp", q_sb[:, :], qT, writes=[r_q])
        fw.dma("sp", k_sb[:, :], kT, writes=[r_k])
        fw.dma("sp", v_sb[:, :, :], vtok.rearrange("(n p) c -> p n c", p=128), writes=[r_v])
        fw.dma("sp", mask[:, :, :], maskd.rearrange("p (i q) -> p i q", i=4), writes=[r_mask])
        fw.dma("sp", ntri[:, :], ntrid, writes=[r_ntri])
        for h in range(2):
            hs = slice(h * 64, (h + 1) * 64)
            for Q in range(8):
                qs = slice(Q * 512, (Q + 1) * 512)
                po, rpo = psO.next()
                prev_sum = None
                nk = 4 * Q + 4
                for idx, kb in enumerate(range(nk - 1, -1, -1)):
                    first = idx == 0
                    diag = kb >= 4 * Q
                    i = kb - 4 * Q
                    pa, rpa = psA.next()
                    fw.op("pe", lambda e, pa=pa, kb=kb: e.matmul(pa[:, :], lhsT=k_sb[hs, kb * 128:(kb + 1) * 128],
                                                                  rhs=q_sb[hs, qs], start=True, stop=True),
                          reads=[r_k, r_q], writes=[rpa])
                    eb, reb = e_p.next()
                    fw.op("act", lambda e, eb=eb, pa=pa: e.activation(out=eb[:, :], in_=pa[:, :], func=AF.Exp, scale=0.125),
                          reads=[rpa], writes=[reb])
                    sp, rsp = sp_p.next()
                    fw.op("act", lambda e, eb=eb, sp=sp: e.activation(out=sp[:, :], in_=eb[:, :], func=AF.Ln, bias=oneb[:, 0:1]),
                          reads=[reb, r_one], writes=[rsp])
                    if diag:
                        fw.op("dve", lambda e, sp=sp, i=i: e.tensor_tensor(out=sp[:, :], in0=sp[:, :], in1=mask[:, i, :], op=ALU.mult),
                              reads=[rsp, r_mask], writes=[rsp])
                    pb, rpb = psB.next()
                    fw.op("pe", lambda e, pb=pb, sp=sp: e.matmul(pb[:, :], lhsT=ntri[:, :], rhs=sp[:, :], start=True, stop=first),
                          reads=[r_ntri, rsp], writes=[rpb])
                    if not first:
                        ps_, rps_ = prev_sum
                        fw.op("pe", lambda e, pb=pb, ps_=ps_: e.matmul(pb[:, :], lhsT=nones[:, :], rhs=ps_[:, :], start=False, stop=True),
                              reads=[r_nones, rps_], writes=[rpb])
                    tb, rtb = t_p.next()
                    fw.op("dve", lambda e, tb=tb, pa=pa, sp=sp: e.scalar_tensor_tensor(out=tb[:, :], in0=pa[:, :], scalar=0.125, in1=sp[:, :],
                                                                                     op0=ALU.mult, op1=ALU.subtract),
                          reads=[rpa, rsp], writes=[rtb])
                    fw.op("dve", lambda e, tb=tb, pb=pb: e.tensor_tensor(out=tb[:, :], in0=tb[:, :], in1=pb[:, :], op=ALU.add),
                          reads=[rtb, rpb], writes=[rtb])
                    wb, rwb = w_p.next()
                    fw.op("act", lambda e, wb=wb, tb=tb: e.activation(out=wb[:, :], in_=tb[:, :], func=AF.Exp),
                          reads=[rtb], writes=[rwb])
                    if diag:
                        fw.op("pool", lambda e, wb=wb, i=i: e.tensor_tensor(out=wb[:, :], in0=wb[:, :], in1=mask[:, i, :], op=ALU.mult),
                              reads=[rwb, r_mask], writes=[rwb])
                    if kb > 0:
                        ns, rns = sum_p.next()
                        if first:
                            fw.op("pool", lambda e, ns=ns, sp=sp: e.tensor_copy(out=ns[:, :], in_=sp[:, :]), reads=[rsp], writes=[rns])
                        else:
                            ps_, rps_ = prev_sum
                            fw.op("pool", lambda e, ns=ns, sp=sp, ps_=ps_: e.tensor_tensor(out=ns[:, :], in0=ps_[:, :], in1=sp[:, :], op=ALU.add),
                                  reads=[rsp, rps_], writes=[rns])
                        prev_sum = (ns, rns)
                    fw.op("pe", lambda e, po=po, wb=wb, kb=kb: e.matmul(po[0:64, :], lhsT=v_sb[:, kb, hs], rhs=wb[:, :],
                                                                        start=first, stop=(kb == 0)),
                          reads=[r_v, rwb], writes=[rpo])
                fw.op("act", lambda e, po=po: e.copy(out=o_sb[h][:, qs], in_=po[0:64, :]), reads=[rpo], writes=[r_o[h]])
            fw.dma("sp", oT[hs, :], o_sb[h][:, :], reads=[r_o[h]], is_output=True)
        fw.finish()
    return nc


def run_sb(qT_list, kT_list, vtok_list):
    nc = _get("sb", build_sb)
    c = sb_consts()
    in_maps = [dict(c, qT=qT_list[i], kT=kT_list[i], vtok=vtok_list[i]) for i in range(NCORE)]
    res = run_bass_kernel_spmd(nc, in_maps, core_ids=list(range(NCORE)))
    return [r["oT"] for r in res.results]


def ret_consts(s):
    t = np.arange(SEQ, dtype=np.float64)
    invf = 10000.0 ** (-np.arange(0, 64, 2, dtype=np.float64) / 64)
    ang = t[None, :] * invf[:, None]
    cos = np.concatenate([np.cos(ang), np.cos(ang)], 0)
    sin = np.concatenate([-np.sin(ang), np.sin(ang)], 0)
    i = (np.arange(SEQ) % 128).astype(np.float64)
    out = {}
    ckt = np.zeros((SEQ, 128)); skt = np.zeros((SEQ, 128))
    for hl in range(2):
        gam = 1.0 - 2.0 ** (-5.0 - (2 * s + hl))
        qd = gam ** (i + 1.0)
        kd2 = gam ** (-(i + 1.0)) / 8.0
        kd = gam ** (127.0 - i) / 8.0
        out["cq%d" % hl] = (cos * qd[None, :]).astype(np.float32)
        out["sq%d" % hl] = (sin * qd[None, :]).astype(np.float32)
        out["ck%d" % hl] = (cos * kd2[None, :]).astype(np.float32)
        out["sk%d" % hl] = (sin * kd2[None, :]).astype(np.float32)
        ckt[:, hl * 64:(hl + 1) * 64] = (cos * kd[None, :]).T
        skt[:, hl * 64:(hl + 1) * 64] = (sin * kd[None, :]).T
        out["gd%d" % hl] = np.full((64, 1), gam ** 128.0, np.float32)
    out["ckt"] = ckt.astype(np.float32)
    out["skt"] = skt.astype(np.float32)
    jj = np.arange(128)[:, None]
    ii = np.arange(128)[None, :]
    out["caus"] = (ii >= jj).astype(np.float32)
    return out


def build_ret(nc=None, ext=None):
    nc = nc or _newnc()
    ext = {} if ext is None else ext
    din = {}
    for nm in ("q", "qs", "k", "ks", "g"):
        din[nm] = _dram(nc, nm + "T", [128, SEQ], BF16, kind="ExternalInput").ap()
    for nm in ("ktok", "kstok", "vtok"):
        din[nm] = _dram(nc, nm, [SEQ, 128], BF16, kind="ExternalInput").ap()
    for hl in range(2):
        for nm in ("cq", "sq", "ck", "sk"):
            din[nm + str(hl)] = _dram(nc, nm + str(hl), [64, SEQ], F32, kind="ExternalInput").ap()
        din["gd%d" % hl] = _dram(nc, "gd%d" % hl, [64, 1], F32, kind="ExternalInput").ap()
    din["ckt"] = _dram(nc, "ckt", [SEQ, 128], F32, kind="ExternalInput").ap()
    din["skt"] = _dram(nc, "skt", [SEQ, 128], F32, kind="ExternalInput").ap()
    din["caus"] = _dram(nc, "caus", [128, 128], F32, kind="ExternalInput").ap()
    oT = _dram(nc, "oT", [128, SEQ], BF16, kind="ExternalOutput").ap()
    with contextlib.ExitStack() as st:
        fw = FW(nc, st)
        inb = Rot([_sb(nc, st, "inb%d" % i, [64, SEQ], BF16) for i in range(2)])
        tab = Rot([_sb(nc, st, "tab%d" % i, [64, SEQ], F32) for i in range(2)])
        tmp = Rot([_sb(nc, st, "tmp%d" % i, [128, 1024], F32) for i in range(2)])
        qd = [_sb(nc, st, "qd%d" % h, [64, SEQ], BF16) for h in range(2)]
        kd2 = [_sb(nc, st, "kd2%d" % h, [64, SEQ], BF16) for h in range(2)]
        sg = [_sb(nc, st, "sg%d" % h, [64, SEQ], BF16) for h in range(2)]
        ob = [_sb(nc, st, "ob%d" % h, [64, SEQ], BF16) for h in range(2)]
        kdt = _sb(nc, st, "kdt", [128, 32, 128], BF16)
        vt = _sb(nc, st, "vt", [128, 32, 128], BF16)
        tkb = Rot([_sb(nc, st, "tkb%d" % i, [128, 8, 128], BF16) for i in range(2)])
        ttb = Rot([_sb(nc, st, "ttb%d" % i, [128, 8, 128], F32) for i in range(2)])
        caus = _sb(nc, st, "caus_sb", [128, 128], F32)
        ones = _sb(nc, st, "ones", [64, 64], F32)
        epsb = _sb(nc, st, "epsb", [64, 1], F32)
        gdec = [_sb(nc, st, "gdec%d" % h, [64, 1], F32) for h in range(2)]
        stf = [_sb(nc, st, "stf%d" % h, [64, 64], F32) for h in range(2)]
        stb = [_sb(nc, st, "stb%d" % h, [64, 64], BF16) for h in range(2)]
        scp = Rot([_sb(nc, st, "sc%d" % i, [128, 128], BF16) for i in range(3)])
        sqb = Rot([_sb(nc, st, "sqb%d" % i, [64, 512], F32) for i in range(2)])
        rsb = Rot([_sb(nc, st, "rsb%d" % i, [64, 512], F32) for i in range(2)])
        psS = Rot([_ps(nc, st, "psS%d" % i, [128, 512]) for i in range(2)])
        psO = Rot([_ps(nc, st, "psO%d" % i, [128, 512]) for i in range(2)])
        psN = Rot([_ps(nc, st, "psN%d" % i, [128, 512]) for i in range(2)])
        psR = Rot([_ps(nc, st, "psR%d" % i, [128, 512]) for i in range(2)])
        r_qd = [Res(), Res()]; r_kd2 = [Res(), Res()]; r_sg = [Res(), Res()]; r_ob = [Res(), Res()]
        r_kdt, r_vt, r_caus, r_ones, r_eps = (Res() for _ in range(5))
        r_gd = [Res(), Res()]; r_stf = [Res(), Res()]; r_stb = [Res(), Res()]
        fw.op("pool", lambda e: e.memset(ones[:, :], 1.0), writes=[r_ones])
        fw.op("pool", lambda e: e.memset(epsb[:, :], EPS), writes=[r_eps])
        fw.dma("sp", caus[:, :], din["caus"], writes=[r_caus])
        fw.dma("sp", vt[:, :, :], din["vtok"].rearrange("(n p) c -> p n c", p=128), writes=[r_vt])
        for h in range(2):
            fw.dma("sp", gdec[h][:, :], din["gd%d" % h], writes=[r_gd[h]])
            fw.op("pool", lambda e, h=h: e.memset(stf[h][:, :], 0.0), writes=[r_stf[h]])
            fw.op("pool", lambda e, h=h: e.memset(stb[h][:, :], 0.0), writes=[r_stb[h]])

        def rope_fm(h, a_nm, b_nm, c_nm, s_nm, dst, r_dst):
            hs = slice(h * 64, (h + 1) * 64)
            a, ra = inb.next(); b, rb = inb.next()
            c, rc = tab.next(); s_, rs = tab.next()
            fw.dma("sp", a[:, :], din[a_nm][hs, :], writes=[ra])
            fw.dma("sp", b[:, :], din[b_nm][hs, :], writes=[rb])
            fw.dma("sp", c[:, :], din[c_nm + str(h)], writes=[rc])
            fw.dma("sp", s_[:, :], din[s_nm + str(h)], writes=[rs])
            for p in range(4):
                sl = slice(p * 1024, (p + 1) * 1024)
                t1, rt1 = tmp.next(); t2, rt2 = tmp.next()
                fw.op("dve", lambda e, t1=t1: e.tensor_tensor(out=t1[0:64, :], in0=a[:, sl], in1=c[:, sl], op=ALU.mult),
                      reads=[ra, rc], writes=[rt1])
                fw.op("pool", lambda e, t2=t2: e.tensor_tensor(out=t2[0:64, :], in0=b[:, sl], in1=s_[:, sl], op=ALU.mult),
                      reads=[rb, rs], writes=[rt2])
                fw.op("dve", lambda e, t1=t1, t2=t2: e.tensor_tensor(out=dst[:, sl], in0=t1[0:64, :], in1=t2[0:64, :], op=ALU.add),
                      reads=[rt1, rt2], writes=[r_dst])

        for h in range(2):
            rope_fm(h, "q", "qs", "cq", "sq", qd[h], r_qd[h])
            rope_fm(h, "k", "ks", "ck", "sk", kd2[h], r_kd2[h])
            hs = slice(h * 64, (h + 1) * 64)
            a, ra = inb.next()
            fw.dma("sp", a[:, :], din["g"][hs, :], writes=[ra])
            fw.op("act", lambda e, a=a, h=h: e.activation(out=sg[h][:, :], in_=a[:, :], func=AF.Silu), reads=[ra], writes=[r_sg[h]])
        ktv = din["ktok"].rearrange("(n p) c -> p n c", p=128)
        ksv = din["kstok"].rearrange("(n p) c -> p n c", p=128)
        ctv = din["ckt"].rearrange("(n p) c -> p n c", p=128)
        stv = din["skt"].rearrange("(n p) c -> p n c", p=128)
        for p in range(4):
            ns = slice(p * 8, (p + 1) * 8)
            a, ra = tkb.next(); b, rb = tkb.next()
            c, rc = ttb.next(); s_, rs = ttb.next()
            fw.dma("sp", a[:, :, :], ktv[:, ns, :], writes=[ra])
            fw.dma("sp", b[:, :, :], ksv[:, ns, :], writes=[rb])
            fw.dma("sp", c[:, :, :], ctv[:, ns, :], writes=[rc])
            fw.dma("sp", s_[:, :, :], stv[:, ns, :], writes=[rs])
            fw.op("dve", lambda e, a=a, c=c: e.tensor_tensor(out=c[:, :, :], in0=a[:, :, :], in1=c[:, :, :], op=ALU.mult),
                  reads=[ra, rc], writes=[rc])
            fw.op("pool", lambda e, b=b, s_=s_: e.tensor_tensor(out=s_[:, :, :], in0=b[:, :, :], in1=s_[:, :, :], op=ALU.mult),
                  reads=[rb, rs], writes=[rs])
            fw.op("dve", lambda e, c=c, s_=s_: e.tensor_tensor(out=kdt[:, ns, :], in0=c[:, :, :], in1=s_[:, :, :], op=ALU.add),
                  reads=[rc, rs], writes=[r_kdt])
        for h in range(2):
            hs = slice(h * 64, (h + 1) * 64)
            for grp in range(8):
                po, rpo = psO.next()
                gsl = slice(grp * 512, (grp + 1) * 512)
                for ci in range(4):
                    n = grp * 4 + ci
                    cs = slice(n * 128, (n + 1) * 128)
                    oc = slice(ci * 128, (ci + 1) * 128)
                    pS, rpS = psS.next()
                    fw.op("pe", lambda e, pS=pS: e.matmul(pS[:, 0:128], lhsT=kd2[h][:, cs], rhs=qd[h][:, cs], start=True, stop=True),
                          reads=[r_kd2[h], r_qd[h]], writes=[rpS])
                    sc, rsc = scp.next()
                    fw.op("dve", lambda e, pS=pS, sc=sc: e.tensor_tensor(out=sc[:, :], in0=pS[:, 0:128], in1=caus[:, :], op=ALU.mult),
                          reads=[rpS, r_caus], writes=[rsc])
                    fw.op("pe", lambda e, sc=sc, n=n: e.matmul(po[0:64, oc], lhsT=vt[:, n, hs], rhs=sc[:, :], start=True, stop=False),
                          reads=[r_vt, rsc], writes=[rpo])
                    fw.op("pe", lambda e: e.matmul(po[0:64, oc], lhsT=stb[h][:, :], rhs=qd[h][:, cs], start=False, stop=True),
                          reads=[r_stb[h], r_qd[h]], writes=[rpo])
                    pN, rpN = psN.next()
                    fw.op("pe", lambda e, pN=pN, n=n: e.matmul(pN[0:64, 0:64], lhsT=kdt[:, n, hs], rhs=vt[:, n, hs], start=True, stop=True),
                          reads=[r_kdt, r_vt], writes=[rpN])
                    fw.op("dve", lambda e, pN=pN: e.scalar_tensor_tensor(out=stf[h][:, :], in0=stf[h][:, :], scalar=gdec[h][:, 0:1],
                                                                        in1=pN[0:64, 0:64], op0=ALU.mult, op1=ALU.add),
                          reads=[rpN, r_stf[h], r_gd[h]], writes=[r_stf[h]])
                    fw.op("act", lambda e: e.copy(out=stb[h][:, :], in_=stf[h][:, :]), reads=[r_stf[h]], writes=[r_stb[h]])
                sq, rsq = sqb.next()
                fw.op("act", lambda e, sq=sq: e.activation(out=sq[:, :], in_=po[0:64, :], func=AF.Square), reads=[rpo], writes=[rsq])
                pR, rpR = psR.next()
                fw.op("pe", lambda e, pR=pR, sq=sq: e.matmul(pR[0:64, :], lhsT=ones[:, :], rhs=sq[:, :], start=True, stop=True),
                      reads=[r_ones, rsq], writes=[rpR])
                rs_, rrs = rsb.next()
                fw.op("act", lambda e, pR=pR, rs_=rs_: e.activation(out=rs_[:, :], in_=pR[0:64, :], func=AF.Ln, bias=epsb[:, 0:1], scale=1.0 / 64),
                      reads=[rpR, r_eps], writes=[rrs])
                fw.op("act", lambda e, rs_=rs_: e.activation(out=rs_[:, :], in_=rs_[:, :], func=AF.Exp, scale=-0.5), reads=[rrs], writes=[rrs])
                fw.op("dve", lambda e, rs_=rs_: e.tensor_tensor(out=rs_[:, :], in0=rs_[:, :], in1=po[0:64, :], op=ALU.mult),
                      reads=[rrs, rpo], writes=[rrs])
                fw.op("pool", lambda e, rs_=rs_: e.tensor_tensor(out=ob[h][:, gsl], in0=rs_[:, :], in1=sg[h][:, gsl], op=ALU.mult),
                      reads=[rrs, r_sg[h]], writes=[r_ob[h]])
            fw.dma("sp", oT[hs, :], ob[h][:, :], reads=[r_ob[h]], is_output=True)
        fw.finish()
    return nc


def run_ret(ins_list):
    nc = _get("ret", build_ret)
    in_maps = []
    for c in range(NCORE):
        m = dict(ret_consts(c % 2))
        m.update(ins_list[c])
        in_maps.append(m)
    res = run_bass_kernel_spmd(nc, in_maps, core_ids=list(range(NCORE)))
    return [r["oT"] for r in res.results]


def s5_host_layout(a_re, a_im, log_dt, b_re, b_im, c_re, c_im, dskip, s):
    gs = slice(8 * s, 8 * s + 8)
    ar = a_re[gs].reshape(4, 128).T
    ai = a_im[gs].reshape(4, 128).T
    ld = np.repeat(log_dt[gs], 64).reshape(4, 128).T
    bb_re = np.zeros((128, 4, 128), np.float32); bb_im = np.zeros((128, 4, 128), np.float32)
    cc_re = np.zeros((128, 4, 128), np.float32); cc_im = np.zeros((128, 4, 128), np.float32)
    for gl in range(8):
        g = 8 * s + gl
        j = gl // 2
        rs = slice((gl % 2) * 64, (gl % 2) * 64 + 64)
        cs = slice(gl * 16, gl * 16 + 16)
        bb_re[rs, j, cs] = b_re[g]
        bb_im[rs, j, cs] = b_im[g]
        cc_re[rs, j, cs] = c_re[g].T
        cc_im[rs, j, cs] = c_im[g].T
    tri = (np.arange(128)[:, None] <= np.arange(128)[None, :]).astype(np.float32)
    return {"s5p": np.ascontiguousarray(np.concatenate([ar, ai, ld], axis=1).astype(np.float32)),
            "bb_re": bb_re.reshape(128, 512), "bb_im": bb_im.reshape(128, 512),
            "cc_re": cc_re.reshape(128, 512), "cc_im": cc_im.reshape(128, 512),
            "dsk": np.ascontiguousarray(dskip[128 * s:128 * s + 128].reshape(128, 1)),
            "tri": tri, "ident": np.eye(128, dtype=np.float32)}


def build_s5(nc=None, ext=None):
    PI = float(np.pi)
    nc = nc or _newnc()
    uT = _dram(nc, "uT", [128, SEQ], BF16, kind="ExternalInput").ap()
    s5p = _dram(nc, "s5p", [128, 12], F32, kind="ExternalInput").ap()
    dd = {nm: _dram(nc, nm, [128, 512], F32, kind="ExternalInput").ap() for nm in ("bb_re", "bb_im", "cc_re", "cc_im")}
    dsk = _dram(nc, "dsk", [128, 1], F32, kind="ExternalInput").ap()
    trid = _dram(nc, "tri", [128, 128], F32, kind="ExternalInput").ap()
    identd = _dram(nc, "ident", [128, 128], F32, kind="ExternalInput").ap()
    yT = _dram(nc, "yT", [128, SEQ], BF16, kind="ExternalOutput").ap()
    with contextlib.ExitStack() as st:
        fw = FW(nc, st)
        cnt = [0]

        def T(shape, dt=F32):
            cnt[0] += 1
            return _sb(nc, st, "t%d" % cnt[0], shape, dt)

        u = T([128, SEQ], BF16); yo = T([128, SEQ], BF16)
        prm = T([128, 12]); dcol = T([128, 1]); tri = T([128, 128]); ident = T([128, 128])
        bb = {k: T([128, 4, 128]) for k in ("bb_re", "bb_im")}
        cc = {k: T([128, 4, 128]) for k in ("cc_re", "cc_im")}
        r_u, r_yo, r_prm, r_d, r_tri, r_id, r_bb, r_cc = (Res() for _ in range(8))
        fw.dma("sp", u[:, :], uT, writes=[r_u])
        fw.dma("sp", prm[:, :], s5p, writes=[r_prm])
        fw.dma("sp", dcol[:, :], dsk, writes=[r_d])
        fw.dma("sp", tri[:, :], trid, writes=[r_tri])
        fw.dma("sp", ident[:, :], identd, writes=[r_id])
        for k in bb:
            fw.dma("sp", bb[k][:, :, :], dd[k].rearrange("p (j c) -> p j c", j=4), writes=[r_bb])
        for k in cc:
            fw.dma("sp", cc[k][:, :, :], dd[k].rearrange("p (j c) -> p j c", j=4), writes=[r_cc])
        R = Res()
        sc = {n: T([128, 4]) for n in ("dt", "x", "th", "mag", "k", "r", "m", "sin", "cos", "abre", "abim", "den", "nre", "zre", "zim",
                                       "t1", "t2", "ire", "iim", "pre", "pim", "qre", "qim", "cre", "cim", "xl_re", "xl_im")}
        ki = T([128, 4], mybir.dt.int32)
        zero_b = T([128, 1]); halfpi = T([128, 1])
        fw.op("pool", lambda e: e.memset(zero_b[:, :], 0.0), writes=[R])
        fw.op("pool", lambda e: e.memset(halfpi[:, :], PI / 2), writes=[R])

        def dv(fn):
            fw.op("dve", fn, reads=[R, r_prm], writes=[R])

        def ac(fn):
            fw.op("act", fn, reads=[R, r_prm], writes=[R])
        a_re = prm[:, 0:4]; a_im = prm[:, 4:8]; ldt = prm[:, 8:12]
        ac(lambda e: e.activation(out=sc["dt"][:, :], in_=ldt, func=AF.Exp))
        dv(lambda e: e.tensor_tensor(out=sc["x"][:, :], in0=a_re, in1=sc["dt"][:, :], op=ALU.mult))
        dv(lambda e: e.tensor_tensor(out=sc["th"][:, :], in0=a_im, in1=sc["dt"][:, :], op=ALU.mult))
        ac(lambda e: e.activation(out=sc["mag"][:, :], in_=sc["x"][:, :], func=AF.Exp))

        def sin_of(dst, shift):
            dv(lambda e: e.tensor_scalar(out=sc["k"][:, :], in0=sc["th"][:, :], scalar1=shift, scalar2=1.0 / (2 * PI), op0=ALU.add, op1=ALU.mult))
            dv(lambda e: e.tensor_copy(out=ki[:, :], in_=sc["k"][:, :]))
            dv(lambda e: e.tensor_copy(out=sc["k"][:, :], in_=ki[:, :]))
            dv(lambda e: e.tensor_scalar(out=sc["r"][:, :], in0=sc["th"][:, :], scalar1=shift, scalar2=None, op0=ALU.add))
            dv(lambda e: e.scalar_tensor_tensor(out=sc["r"][:, :], in0=sc["k"][:, :], scalar=-2 * PI, in1=sc["r"][:, :], op0=ALU.mult, op1=ALU.add))
            dv(lambda e: e.tensor_scalar(out=sc["m"][:, :], in0=sc["r"][:, :], scalar1=PI, scalar2=-2 * PI, op0=ALU.is_gt, op1=ALU.mult))
            dv(lambda e: e.tensor_tensor(out=sc["r"][:, :], in0=sc["r"][:, :], in1=sc["m"][:, :], op=ALU.add))
            dv(lambda e: e.tensor_scalar(out=sc["m"][:, :], in0=sc["r"][:, :], scalar1=-PI, scalar2=2 * PI, op0=ALU.is_lt, op1=ALU.mult))
            dv(lambda e: e.tensor_tensor(out=sc["r"][:, :], in0=sc["r"][:, :], in1=sc["m"][:, :], op=ALU.add))
            ac(lambda e: e.activation(out=sc[dst][:, :], in_=sc["r"][:, :], func=AF.Sin))
        sin_of("sin", 0.0)
        sin_of("cos", PI / 2)
        dv(lambda e: e.tensor_tensor(out=sc["abre"][:, :], in0=sc["mag"][:, :], in1=sc["cos"][:, :], op=ALU.mult))
        dv(lambda e: e.tensor_tensor(out=sc["abim"][:, :], in0=sc["mag"][:, :], in1=sc["sin"][:, :], op=ALU.mult))
        dv(lambda e: e.tensor_tensor(out=sc["den"][:, :], in0=a_re, in1=a_re, op=ALU.mult))
        dv(lambda e: e.tensor_tensor(out=sc["t1"][:, :], in0=a_im, in1=a_im, op=ALU.mult))
        dv(lambda e: e.tensor_tensor(out=sc["den"][:, :], in0=sc["den"][:, :], in1=sc["t1"][:, :], op=ALU.add))
        dv(lambda e: e.reciprocal(out=sc["den"][:, :], in_=sc["den"][:, :]))
        dv(lambda e: e.tensor_scalar(out=sc["nre"][:, :], in0=sc["abre"][:, :], scalar1=-1.0, scalar2=None, op0=ALU.add))
        dv(lambda e: e.tensor_tensor(out=sc["t1"][:, :], in0=sc["nre"][:, :], in1=a_re, op=ALU.mult))
        dv(lambda e: e.tensor_tensor(out=sc["t2"][:, :], in0=sc["abim"][:, :], in1=a_im, op=ALU.mult))
        dv(lambda e: e.tensor_tensor(out=sc["t1"][:, :], in0=sc["t1"][:, :], in1=sc["t2"][:, :], op=ALU.add))
        dv(lambda e: e.tensor_tensor(out=sc["zre"][:, :], in0=sc["t1"][:, :], in1=sc["den"][:, :], op=ALU.mult))
        dv(lambda e: e.tensor_tensor(out=sc["t1"][:, :], in0=sc["abim"][:, :], in1=a_re, op=ALU.mult))
        dv(lambda e: e.tensor_tensor(out=sc["t2"][:, :], in0=sc["nre"][:, :], in1=a_im, op=ALU.mult))
        dv(lambda e: e.tensor_tensor(out=sc["t1"][:, :], in0=sc["t1"][:, :], in1=sc["t2"][:, :], op=ALU.subtract))
        dv(lambda e: e.tensor_tensor(out=sc["zim"][:, :], in0=sc["t1"][:, :], in1=sc["den"][:, :], op=ALU.mult))
        dv(lambda e: e.tensor_tensor(out=sc["t1"][:, :], in0=sc["mag"][:, :], in1=sc["mag"][:, :], op=ALU.mult))
        dv(lambda e: e.reciprocal(out=sc["t1"][:, :], in_=sc["t1"][:, :]))
        dv(lambda e: e.tensor_tensor(out=sc["ire"][:, :], in0=sc["abre"][:, :], in1=sc["t1"][:, :], op=ALU.mult))
        dv(lambda e: e.scalar_tensor_tensor(out=sc["iim"][:, :], in0=sc["abim"][:, :], scalar=-1.0, in1=sc["t1"][:, :], op0=ALU.mult, op1=ALU.mult))

        def build_pow(bre, bim):
            Er = T([128, 4, 128]); Ei = T([128, 4, 128]); tt = T([128, 4, 64])
            dv(lambda e: e.memset(Er[:, :, 0:1], 1.0))
            dv(lambda e: e.memset(Ei[:, :, 0:1], 0.0))
            dv(lambda e: e.tensor_copy(out=sc["pre"][:, :], in_=sc[bre][:, :]))
            dv(lambda e: e.tensor_copy(out=sc["pim"][:, :], in_=sc[bim][:, :]))
            m = 1
            while m < 128:
                for j in range(4):
                    pr = sc["pre"][:, j:j + 1]; pi_ = sc["pim"][:, j:j + 1]
                    dv(lambda e, j=j, pi_=pi_: e.tensor_scalar(out=tt[:, j, 0:m], in0=Ei[:, j, 0:m], scalar1=pi_, scalar2=None, op0=ALU.mult))
                    dv(lambda e, j=j, pr=pr: e.scalar_tensor_tensor(out=Er[:, j, m:2 * m], in0=Er[:, j, 0:m], scalar=pr, in1=tt[:, j, 0:m],
                                                                      op0=ALU.mult, op1=ALU.subtract))
                    dv(lambda e, j=j, pi_=pi_: e.tensor_scalar(out=tt[:, j, 0:m], in0=Er[:, j, 0:m], scalar1=pi_, scalar2=None, op0=ALU.mult))
                    dv(lambda e, j=j, pr=pr: e.scalar_tensor_tensor(out=Ei[:, j, m:2 * m], in0=Ei[:, j, 0:m], scalar=pr, in1=tt[:, j, 0:m],
                                                                      op0=ALU.mult, op1=ALU.add))
                dv(lambda e: e.tensor_tensor(out=sc["qre"][:, :], in0=sc["pre"][:, :], in1=sc["pre"][:, :], op=ALU.mult))
                dv(lambda e: e.tensor_tensor(out=sc["qim"][:, :], in0=sc["pim"][:, :], in1=sc["pim"][:, :], op=ALU.mult))
                dv(lambda e: e.tensor_tensor(out=sc["qre"][:, :], in0=sc["qre"][:, :], in1=sc["qim"][:, :], op=ALU.subtract))
                dv(lambda e: e.tensor_tensor(out=sc["qim"][:, :], in0=sc["pre"][:, :], in1=sc["pim"][:, :], op=ALU.mult))
                dv(lambda e: e.tensor_scalar(out=sc["pim"][:, :], in0=sc["qim"][:, :], scalar1=2.0, scalar2=None, op0=ALU.mult))
                dv(lambda e: e.tensor_copy(out=sc["pre"][:, :], in_=sc["qre"][:, :]))
                m *= 2
            return Er, Ei
        Epr, Epi = build_pow("abre", "abim")
        Enr_s, Eni_s = build_pow("ire", "iim")
        psp = Rot([_ps(nc, st, "ps%d" % i, [128, 512]) for i in range(6)])
        psy = Rot([_ps(nc, st, "psy%d" % i, [128, 512]) for i in range(2)])
        Enr = T([128, 512]); Eni = T([128, 512])
        for (src, dst) in ((Enr_s, Enr), (Eni_s, Eni)):
            p, rp = psp.next()
            for j in range(4):
                fw.op("pe", lambda e, j=j, p=p, src=src: e.matmul(p[:, j * 128:(j + 1) * 128], lhsT=src[:, j, :], rhs=ident[:, :], start=True, stop=True),
                      reads=[R, r_id], writes=[rp])
            fw.op("act", lambda e, p=p, dst=dst: e.copy(out=dst[:, :], in_=p[:, :]), reads=[rp], writes=[R])
        Bmat = T([128, 1024], BF16)
        bsr = T([128, 4, 128]); bsi = T([128, 4, 128]); btt = T([128, 128])
        for j in range(4):
            zr = sc["zre"][:, j:j + 1]; zi = sc["zim"][:, j:j + 1]
            fw.op("dve", lambda e, j=j, zi=zi: e.tensor_scalar(out=btt[:, :], in0=bb["bb_im"][:, j, :], scalar1=zi, scalar2=None, op0=ALU.mult), reads=[R, r_bb], writes=[R])
            fw.op("dve", lambda e, j=j, zr=zr: e.scalar_tensor_tensor(out=bsr[:, j, :], in0=bb["bb_re"][:, j, :], scalar=zr, in1=btt[:, :], op0=ALU.mult, op1=ALU.subtract),
                  reads=[R, r_bb], writes=[R])
            fw.op("dve", lambda e, j=j, zi=zi: e.tensor_scalar(out=btt[:, :], in0=bb["bb_re"][:, j, :], scalar1=zi, scalar2=None, op0=ALU.mult), reads=[R, r_bb], writes=[R])
            fw.op("dve", lambda e, j=j, zr=zr: e.scalar_tensor_tensor(out=bsi[:, j, :], in0=bb["bb_im"][:, j, :], scalar=zr, in1=btt[:, :], op0=ALU.mult, op1=ALU.add),
                  reads=[R, r_bb], writes=[R])
        for ri, src in enumerate((bsr, bsi)):
            p, rp = psp.next()
            for j in range(4):
                fw.op("pe", lambda e, j=j, p=p, src=src: e.matmul(p[:, j * 128:(j + 1) * 128], lhsT=src[:, j, :], rhs=ident[:, :], start=True, stop=True),
                      reads=[R, r_id], writes=[rp])
            fw.op("act", lambda e, p=p, ri=ri: e.copy(out=Bmat[:, ri * 512:(ri + 1) * 512], in_=p[:, :]), reads=[rp], writes=[R])
        fw.op("dve", lambda e: e.tensor_scalar(out=cc["cc_im"][:, :, :], in0=cc["cc_im"][:, :, :], scalar1=-1.0, scalar2=None, op0=ALU.mult),
              reads=[r_cc], writes=[r_cc])
        dv(lambda e: e.memset(sc["cre"][:, :], 0.0))
        dv(lambda e: e.memset(sc["cim"][:, :], 0.0))
        bu_p = Rot([T([128, 1024]) for _ in range(2)])
        z_p = Rot([T([128, 1024]) for _ in range(2)])
        zt_p = Rot([T([128, 512]) for _ in range(4)])
        wc_p = Rot([T([128, 1024]) for _ in range(2)])
        x_p = Rot([T([128, 1024]) for _ in range(2)])
        xt_p = Rot([T([128, 512]) for _ in range(4)])
        g_p = Rot([T([128, 512]) for _ in range(3)])
        r_c = Res()
        py, rpy = None, None
        for n in range(32):
            cs = slice(n * 128, (n + 1) * 128)
            pb0, rpb0 = psp.next(); pb1, rpb1 = psp.next()
            fw.op("pe", lambda e, pb0=pb0: e.matmul(pb0[:, :], lhsT=u[:, cs], rhs=Bmat[:, 0:512], start=True, stop=True), reads=[r_u, R], writes=[rpb0])
            fw.op("pe", lambda e, pb1=pb1: e.matmul(pb1[:, :], lhsT=u[:, cs], rhs=Bmat[:, 512:1024], start=True, stop=True), reads=[r_u, R], writes=[rpb1])
            bu, rbu = bu_p.next()
            fw.op("act", lambda e, bu=bu, pb0=pb0: e.copy(out=bu[:, 0:512], in_=pb0[:, :]), reads=[rpb0], writes=[rbu])
            fw.op("act", lambda e, bu=bu, pb1=pb1: e.copy(out=bu[:, 512:1024], in_=pb1[:, :]), reads=[rpb1], writes=[rbu])
            z, rz = z_p.next()
            ta, rta = zt_p.next(); tb, rtb = zt_p.next()
            fw.op("dve", lambda e, ta=ta, bu=bu: e.tensor_tensor(out=ta[:, :], in0=Eni[:, :], in1=bu[:, 512:1024], op=ALU.mult), reads=[rbu, R], writes=[rta])
            fw.op("pool", lambda e, tb=tb, bu=bu: e.tensor_tensor(out=tb[:, :], in0=Eni[:, :], in1=bu[:, 0:512], op=ALU.mult), reads=[rbu, R], writes=[rtb])
            fw.op("dve", lambda e, z=z, bu=bu: e.tensor_tensor(out=z[:, 0:512], in0=Enr[:, :], in1=bu[:, 0:512], op=ALU.mult), reads=[rbu, R], writes=[rz])
            fw.op("pool", lambda e, z=z, bu=bu: e.tensor_tensor(out=z[:, 512:1024], in0=Enr[:, :], in1=bu[:, 512:1024], op=ALU.mult), reads=[rbu, R], writes=[rz])
            fw.op("dve", lambda e, z=z, ta=ta: e.tensor_tensor(out=z[:, 0:512], in0=z[:, 0:512], in1=ta[:, :], op=ALU.subtract), reads=[rz, rta], writes=[rz])
            fw.op("pool", lambda e, z=z, tb=tb: e.tensor_tensor(out=z[:, 512:1024], in0=z[:, 512:1024], in1=tb[:, :], op=ALU.add), reads=[rz, rtb], writes=[rz])
            pwr, rpwr = psp.next(); pwi, rpwi = psp.next()
            for j in range(4):
                fw.op("pe", lambda e, j=j, pwr=pwr, z=z: e.matmul(pwr[:, j * 128:(j + 1) * 128], lhsT=z[:, j * 128:(j + 1) * 128], rhs=tri[:, :], start=True, stop=True),
                      reads=[rz, r_tri], writes=[rpwr])
                fw.op("pe", lambda e, j=j, pwi=pwi, z=z: e.matmul(pwi[:, j * 128:(j + 1) * 128], lhsT=z[:, 512 + j * 128:512 + (j + 1) * 128], rhs=tri[:, :], start=True, stop=True),
                      reads=[rz, r_tri], writes=[rpwi])
            wc, rwc = wc_p.next()
            for j in range(4):
                fw.op("act", lambda e, j=j, wc=wc, pwr=pwr: e.activation(out=wc[:, j * 128:(j + 1) * 128], in_=pwr[:, j * 128:(j + 1) * 128], func=AF.Identity,
                                                                       bias=sc["cre"][:, j:j + 1]), reads=[rpwr, r_c, R], writes=[rwc])
                fw.op("act", lambda e, j=j, wc=wc, pwi=pwi: e.activation(out=wc[:, 512 + j * 128:512 + (j + 1) * 128], in_=pwi[:, j * 128:(j + 1) * 128], func=AF.Identity,
                                                                       bias=sc["cim"][:, j:j + 1]), reads=[rpwi, r_c, R], writes=[rwc])
            x, rx = x_p.next()
            ta, rta = xt_p.next(); tb, rtb = xt_p.next()
            Eprf = Epr[:, :, :].rearrange("p j t -> p (j t)"); Epif = Epi[:, :, :].rearrange("p j t -> p (j t)")
            fw.op("dve", lambda e, ta=ta, wc=wc: e.tensor_tensor(out=ta[:, :], in0=Epif, in1=wc[:, 512:1024], op=ALU.mult), reads=[rwc, R], writes=[rta])
            fw.op("pool", lambda e, tb=tb, wc=wc: e.tensor_tensor(out=tb[:, :], in0=Epif, in1=wc[:, 0:512], op=ALU.mult), reads=[rwc, R], writes=[rtb])
            fw.op("dve", lambda e, x=x, wc=wc: e.tensor_tensor(out=x[:, 0:512], in0=Eprf, in1=wc[:, 0:512], op=ALU.mult), reads=[rwc, R], writes=[rx])
            fw.op("pool", lambda e, x=x, wc=wc: e.tensor_tensor(out=x[:, 512:1024], in0=Eprf, in1=wc[:, 512:1024], op=ALU.mult), reads=[rwc, R], writes=[rx])
            fw.op("dve", lambda e, x=x, ta=ta: e.tensor_tensor(out=x[:, 0:512], in0=x[:, 0:512], in1=ta[:, :], op=ALU.subtract), reads=[rx, rta], writes=[rx])
            fw.op("pool", lambda e, x=x, tb=tb: e.tensor_tensor(out=x[:, 512:1024], in0=x[:, 512:1024], in1=tb[:, :], op=ALU.add), reads=[rx, rtb], writes=[rx])
            if n < 31:
                xlr = x[:, 0:512].rearrange("p (j t) -> p j t", j=4)[:, :, 127]
                xli = x[:, 512:1024].rearrange("p (j t) -> p j t", j=4)[:, :, 127]
                fw.op("dve", lambda e, xlr=xlr: e.tensor_copy(out=sc["xl_re"][:, :], in_=xlr), reads=[rx, R], writes=[R])
                fw.op("dve", lambda e, xli=xli: e.tensor_copy(out=sc["xl_im"][:, :], in_=xli), reads=[rx, R], writes=[R])
                dv(lambda e: e.tensor_tensor(out=sc["t1"][:, :], in0=sc["abim"][:, :], in1=sc["xl_im"][:, :], op=ALU.mult))
                dv(lambda e: e.tensor_tensor(out=sc["t2"][:, :], in0=sc["abim"][:, :], in1=sc["xl_re"][:, :], op=ALU.mult))
                fw.op("dve", lambda e: e.tensor_tensor(out=sc["cre"][:, :], in0=sc["abre"][:, :], in1=sc["xl_re"][:, :], op=ALU.mult), reads=[R, r_c], writes=[r_c])
                fw.op("dve", lambda e: e.tensor_tensor(out=sc["cim"][:, :], in0=sc["abre"][:, :], in1=sc["xl_im"][:, :], op=ALU.mult), reads=[R, r_c], writes=[r_c])
                fw.op("dve", lambda e: e.tensor_tensor(out=sc["cre"][:, :], in0=sc["cre"][:, :], in1=sc["t1"][:, :], op=ALU.subtract), reads=[R, r_c], writes=[r_c])
                fw.op("dve", lambda e: e.tensor_tensor(out=sc["cim"][:, :], in0=sc["cim"][:, :], in1=sc["t2"][:, :], op=ALU.add), reads=[R, r_c], writes=[r_c])
            if n % 4 == 0:
                py, rpy = psy.next()
            oc = slice((n % 4) * 128, (n % 4 + 1) * 128)
            for j in range(4):
                fw.op("pe", lambda e, j=j, x=x, py=py: e.matmul(py[:, oc], lhsT=cc["cc_re"][:, j, :], rhs=x[:, j * 128:(j + 1) * 128], start=(j == 0), stop=False),
                      reads=[r_cc, rx], writes=[rpy])
            for j in range(4):
                fw.op("pe", lambda e, j=j, x=x, py=py: e.matmul(py[:, oc], lhsT=cc["cc_im"][:, j, :], rhs=x[:, 512 + j * 128:512 + (j + 1) * 128], start=False, stop=(j == 3)),
                      reads=[r_cc, rx], writes=[rpy])
            if n % 4 == 3:
                gs = slice((n // 4) * 512, (n // 4 + 1) * 512)
                ya, rya = g_p.next(); yb, ryb = g_p.next()
                fw.op("dve", lambda e, ya=ya, py=py: e.scalar_tensor_tensor(out=ya[:, :], in0=u[:, gs], scalar=dcol[:, 0:1], in1=py[:, :], op0=ALU.mult, op1=ALU.add),
                      reads=[r_u, r_d, rpy], writes=[rya])
                fw.op("pool", lambda e, ya=ya, yb=yb: e.tensor_tensor(out=yb[:, :], in0=ya[:, :], in1=ya[:, :], op=ALU.mult), reads=[rya], writes=[ryb])
                fw.op("dve", lambda e, yb=yb: e.tensor_scalar(out=yb[:, :], in0=yb[:, :], scalar1=0.044715, scalar2=1.0, op0=ALU.mult, op1=ALU.add), reads=[ryb], writes=[ryb])
                fw.op("pool", lambda e, ya=ya, yb=yb: e.tensor_tensor(out=yb[:, :], in0=yb[:, :], in1=ya[:, :], op=ALU.mult), reads=[rya, ryb], writes=[ryb])
                fw.op("act", lambda e, yb=yb: e.activation(out=yb[:, :], in_=yb[:, :], func=AF.Tanh, scale=float(np.sqrt(2.0 / np.pi))), reads=[ryb], writes=[ryb])
                fw.op("dve", lambda e, ya=ya, yb=yb: e.scalar_tensor_tensor(out=yb[:, :], in0=yb[:, :], scalar=1.0, in1=ya[:, :], op0=ALU.add, op1=ALU.mult), reads=[rya, ryb], writes=[ryb])
                fw.op("pool", lambda e, yb=yb: e.tensor_scalar(out=yo[:, gs], in0=yb[:, :], scalar1=0.5, scalar2=None, op0=ALU.mult), reads=[ryb], writes=[r_yo])
        fw.dma("sp", yT, yo[:, :], reads=[r_yo], is_output=True)
        fw.finish()
    return nc


def run_s5(uT_list, lay_list):
    nc = _get("s5", build_s5)
    in_maps = [dict(lay_list[c], uT=uT_list[c]) for c in range(NCORE)]
    res = run_bass_kernel_spmd(nc, in_maps, core_ids=list(range(NCORE)))
    return [r["yT"] for r in res.results]


GN_EPS = 64e-5


def rw_consts():
    s = np.arange(128)[:, None]
    t = np.arange(128)[None, :]
    strict = (s < t).astype(np.float32)
    incl = (s <= t).astype(np.float32)
    mask4 = np.concatenate([strict, strict, incl, incl], axis=1)
    maskL = (t < s).astype(np.float32)
    bones = np.kron(np.eye(2), np.ones((64, 64))).astype(np.float32)
    return {"mask4": np.ascontiguousarray(mask4), "maskL": np.ascontiguousarray(maskL), "bones": bones,
            "ident": np.eye(128, dtype=np.float32)}


def rw_host_params(inp, l, s):
    cs = slice(128 * s, 128 * s + 128)
    mu = inp["rw_mu"][l]
    prm = np.zeros((128, 16), np.float32)
    prm[:, 0] = mu[0:256][cs]; prm[:, 1] = mu[256:512][cs]; prm[:, 2] = mu[512:768][cs]
    prm[:, 3] = mu[896:1024]
    prm[:, 4] = inp["rw_w0"][l][cs]; prm[:, 5] = inp["rw_a0"][l][cs]
    prm[:, 7] = inp["rw_k_k"][l][cs]; prm[:, 8] = inp["rw_k_a"][l][cs]; prm[:, 9] = inp["rw_r_k"][l][cs]
    prm[:, 10] = inp["rw_ln_w"][l][cs]; prm[:, 11] = inp["rw_ln_b"][l][cs]
    prm[0:64, 12] = mu[768:832]; prm[0:64, 13] = mu[832:896]
    out = {"prm": prm, "w2": np.ascontiguousarray(inp["rw_w2"][l][:, cs]), "a2": np.ascontiguousarray(inp["rw_a2"][l][:, cs]),
           "g2": np.ascontiguousarray(inp["rw_g2"][l][:, cs])}
    if l > 0:
        prm[:, 6] = inp["rw_v0"][l - 1][cs]
        prm[0:32, 14] = inp["rw_vres_mu"][l - 1]
        out["v2"] = np.ascontiguousarray(inp["rw_v2"][l - 1][:, cs])
    return out


def build_rw(first, nc=None):
    nc = nc or _newnc()
    W = 1024
    d_in = {}
    for nm, rows in (("r", 128), ("k", 128), ("v", 128), ("gd", 128), ("wd", 64), ("ad", 64)):
        d_in[nm] = _dram(nc, nm + "T", [rows, SEQ], BF16, kind="ExternalInput").ap()
    if not first:
        d_in["vr"] = _dram(nc, "vrT", [32, SEQ], BF16, kind="ExternalInput").ap()
        d_in["vf"] = _dram(nc, "vfT", [128, SEQ], BF16, kind="ExternalInput").ap()
        v2d = _dram(nc, "v2", [32, 128], F32, kind="ExternalInput").ap()
    prmd = _dram(nc, "prm", [128, 16], F32, kind="ExternalInput").ap()
    w2d = _dram(nc, "w2", [64, 128], F32, kind="ExternalInput").ap()
    a2d = _dram(nc, "a2", [64, 128], F32, kind="ExternalInput").ap()
    g2d = _dram(nc, "g2", [128, 128], F32, kind="ExternalInput").ap()
    cd = {nm: _dram(nc, nm, sh, F32, kind="ExternalInput").ap()
          for nm, sh in (("mask4", [128, 512]), ("maskL", [128, 128]), ("bones", [128, 128]), ("ident", [128, 128]))}
    oT = _dram(nc, "oT", [128, SEQ], BF16, kind="ExternalOutput").ap()
    if first:
        vfo = _dram(nc, "vfo", [128, SEQ], BF16, kind="ExternalOutput").ap()
    with contextlib.ExitStack() as st:
        fw = FW(nc, st)
        cnt = [0]

        def T(shape, dt=F32):
            cnt[0] += 1
            return _sb(nc, st, "t%d" % cnt[0], shape, dt)
        prm = T([128, 16]); w2s = T([64, 128], BF16); a2s = T([64, 128], BF16); g2s = T([128, 128], BF16)
        mask4 = T([128, 512]); maskL = T([128, 128]); bones = T([128, 128]); ident = T([128, 128])
        cst = T([128, 4])
        RC = Res()
        fw.dma("sp", prm[:, :], prmd, writes=[RC])
        fw.dma("pool", w2s[:, :], w2d, writes=[RC])
        fw.dma("pool", a2s[:, :], a2d, writes=[RC])
        fw.dma("pool", g2s[:, :], g2d, writes=[RC])
        if not first:
            v2s = T([32, 128], BF16)
            fw.dma("pool", v2s[:, :], v2d, writes=[RC])
        for nm, tl in (("mask4", mask4), ("maskL", maskL), ("bones", bones), ("ident", ident)):
            fw.dma("sp", tl[:, :], cd[nm], writes=[RC])
        fw.op("dve", lambda e: e.memset(cst[:, 0:1], 1.0), writes=[RC])
        fw.op("dve", lambda e: e.memset(cst[:, 1:2], -0.5), reads=[RC], writes=[RC])
        fw.op("dve", lambda e: e.memset(cst[:, 2:3], GN_EPS), reads=[RC], writes=[RC])
        fw.op("dve", lambda e: e.tensor_scalar(out=cst[:, 3:4], in0=prm[:, 4:5], scalar1=-1.0, scalar2=None, op0=ALU.mult), reads=[RC], writes=[RC])
        ST = T([128, 64])
        r_ST = Res()
        fw.op("dve", lambda e: e.memset(ST[:, :], 0.0), writes=[r_ST])
        inb = {nm: T([128 if nm in ("r", "k", "v", "gd") else 64, W + 1], BF16) for nm in ("r", "k", "v", "gd", "wd", "ad")}
        if not first:
            inb["vr"] = T([32, W + 1], BF16)
            vfb = T([128, W], BF16)
        names = ["d", "xr", "xk", "xv", "nlw", "a", "g", "kk", "k2", "cumA", "cumB", "rt", "at", "bt", "kt", "ynT", "bon", "en", "tmp"]
        S = {nm: T([128, W]) for nm in names}
        twd = T([64, W], BF16); xad = T([64, W], BF16); sgd = T([128, W], BF16)
        if not first:
            xvr = T([32, W], BF16)
        ob = T([128, W], BF16)
        vob = T([128, W], BF16)
        PLs = T([128, 8])
        R = Res()
        r_in = Res(); r_ob = Res(); r_vob = Res()
        psA = Rot([_ps(nc, st, "psA%d" % i, [128, 512]) for i in range(2)])
        psN = Rot([_ps(nc, st, "psN%d" % i, [128, 512]) for i in range(2)])
        psC = Rot([_ps(nc, st, "psC%d" % i, [128, 512]) for i in range(2)])
        psX = Rot([_ps(nc, st, "psX%d" % i, [128, 512]) for i in range(2)])
        bh_p = Rot([T([128, 128]) for _ in range(2)]); kh_p = Rot([T([128, 128]) for _ in range(2)])
        tok_p = Rot([T([128, 3, 128]) for _ in range(2)])
        AT_p = Rot([T([128, 4, 128]) for _ in range(2)])
        NN_p = Rot([T([128, 2, 128]) for _ in range(3)])
        Tt_p = Rot([T([128, 128]) for _ in range(3)])
        x0_p = Rot([T([128, 64]) for _ in range(2)]); u_p = Rot([T([128, 64]) for _ in range(2)])
        y_p = Rot([T([128, 64]) for _ in range(2)]); yc_p = Rot([T([128, 64]) for _ in range(2)])
        ynb_p = Rot([T([128, 128]) for _ in range(2)])
        st_p = Rot([T([128, 8]) for _ in range(4)])

        def dv(fn, eng="dve"):
            fw.op(eng, fn, reads=[R, RC, r_in], writes=[R])

        def shiftmix(src, rows, mucol, dst):
            dv(lambda e: e.tensor_tensor(out=S["d"][0:rows, :], in0=src[0:rows, 0:W], in1=src[0:rows, 1:W + 1], op=ALU.subtract))
            dv(lambda e: e.scalar_tensor_tensor(out=dst, in0=S["d"][0:rows, :], scalar=prm[0:rows, mucol:mucol + 1], in1=src[0:rows, 1:W + 1],
                                                op0=ALU.mult, op1=ALU.add))

        for seg in range(SEQ // W):
            t0 = seg * W
            for nm, tl in inb.items():
                rows = tl.shape[0]
                if seg == 0:
                    fw.op("dve", lambda e, tl=tl, rows=rows: e.memset(tl[0:rows, 0:1], 0.0), reads=[R], writes=[r_in])
                    fw.dma("sp", tl[0:rows, 1:W + 1], d_in[nm][:, 0:W], reads=[R], writes=[r_in])
                else:
                    fw.dma("sp", tl[0:rows, :], d_in[nm][:, t0 - 1:t0 + W], reads=[R], writes=[r_in])
            if not first:
                fw.dma("sp", vfb[:, :], d_in["vf"][:, t0:t0 + W], reads=[R], writes=[r_in])
            shiftmix(inb["r"], 128, 0, S["xr"][:, :])
            shiftmix(inb["k"], 128, 1, S["xk"][:, :])
            shiftmix(inb["v"], 128, 2, S["xv"][:, :])
            shiftmix(inb["wd"], 64, 12, S["tmp"][0:64, :])
            fw.op("act", lambda e: e.activation(out=twd[:, :], in_=S["tmp"][0:64, :], func=AF.Tanh), reads=[R], writes=[R])
            shiftmix(inb["ad"], 64, 13, xad[:, :])
            shiftmix(inb["gd"], 128, 3, S["tmp"][:, :])
            fw.op("act", lambda e: e.activation(out=sgd[:, :], in_=S["tmp"][:, :], func=AF.Sigmoid), reads=[R], writes=[R])
            if not first:
                shiftmix(inb["vr"], 32, 14, xvr[:, :])
            for blk in range(W // 512):
                bs = slice(blk * 512, (blk + 1) * 512)
                p, rp = psA.next()
                fw.op("pe", lambda e, p=p: e.matmul(p[:, :], lhsT=w2s[:, :], rhs=twd[:, bs], start=True, stop=True), reads=[R, RC], writes=[rp])
                fw.op("act", lambda e, p=p: e.activation(out=S["tmp"][:, bs], in_=p[:, :], func=AF.Exp, scale=-1.0, bias=cst[:, 3:4]), reads=[rp, R, RC], writes=[R])
                fw.op("act", lambda e: e.activation(out=S["tmp"][:, bs], in_=S["tmp"][:, bs], func=AF.Ln, bias=cst[:, 0:1]), reads=[R, RC], writes=[R])
                fw.op("act", lambda e: e.activation(out=S["nlw"][:, bs], in_=S["tmp"][:, bs], func=AF.Exp, scale=-1.0, bias=cst[:, 1:2]), reads=[R, RC], writes=[R])
                p, rp = psA.next()
                fw.op("pe", lambda e, p=p: e.matmul(p[:, :], lhsT=a2s[:, :], rhs=xad[:, bs], start=True, stop=True), reads=[R, RC], writes=[rp])
                fw.op("act", lambda e, p=p: e.activation(out=S["a"][:, bs], in_=p[:, :], func=AF.Sigmoid, bias=prm[:, 5:6]), reads=[rp, R, RC], writes=[R])
                p, rp = psN.next()
                fw.op("pe", lambda e, p=p: e.matmul(p[:, :], lhsT=g2s[:, :], rhs=sgd[:, bs], start=True, stop=True), reads=[R, RC], writes=[rp])
                fw.op("act", lambda e, p=p: e.copy(out=S["g"][:, bs], in_=p[:, :]), reads=[rp, R], writes=[R])
                if not first:
                    p, rp = psN.next()
                    fw.op("pe", lambda e, p=p: e.matmul(p[:, :], lhsT=v2s[:, :], rhs=xvr[:, bs], start=True, stop=True), reads=[R, RC], writes=[rp])
                    fw.op("act", lambda e, p=p: e.activation(out=S["tmp"][:, bs], in_=p[:, :], func=AF.Sigmoid, bias=prm[:, 6:7]), reads=[rp, R, RC], writes=[R])
                    dv(lambda e: e.tensor_tensor(out=S["d"][:, bs], in0=vfb[:, bs], in1=S["xv"][:, bs], op=ALU.subtract))
                    dv(lambda e: e.tensor_tensor(out=S["d"][:, bs], in0=S["d"][:, bs], in1=S["tmp"][:, bs], op=ALU.mult))
                    dv(lambda e: e.tensor_tensor(out=S["xv"][:, bs], in0=S["xv"][:, bs], in1=S["d"][:, bs], op=ALU.add))
            if first:
                fw.op("act", lambda e: e.copy(out=vob[:, :], in_=S["xv"][:, :]), reads=[R], writes=[r_vob])
                fw.dma("sp", vfo[:, t0:t0 + W], vob[:, :], reads=[r_vob], is_output=True)
            dv(lambda e: e.tensor_scalar(out=S["kk"][:, :], in0=S["xk"][:, :], scalar1=prm[:, 7:8], scalar2=None, op0=ALU.mult))
            dv(lambda e: e.tensor_tensor(out=S["d"][:, :], in0=S["kk"][:, :], in1=S["kk"][:, :], op=ALU.mult), eng="pool")
            for blk in range(W // 512):
                bs = slice(blk * 512, (blk + 1) * 512)
                p, rp = psA.next()
                fw.op("pe", lambda e, p=p: e.matmul(p[:, :], lhsT=bones[:, :], rhs=S["d"][:, bs], start=True, stop=True), reads=[R, RC], writes=[rp])
                fw.op("dve", lambda e, p=p: e.tensor_scalar(out=S["tmp"][:, bs], in0=p[:, :], scalar1=1e-12, scalar2=None, op0=ALU.max), reads=[rp, R], writes=[R])
            fw.op("act", lambda e: e.activation(out=S["tmp"][:, :], in_=S["tmp"][:, :], func=AF.Ln), reads=[R], writes=[R])
            fw.op("act", lambda e: e.activation(out=S["tmp"][:, :], in_=S["tmp"][:, :], func=AF.Exp, scale=-0.5), reads=[R], writes=[R])
            dv(lambda e: e.tensor_tensor(out=S["kk"][:, :], in0=S["kk"][:, :], in1=S["tmp"][:, :], op=ALU.mult))
            dv(lambda e: e.tensor_scalar(out=S["d"][:, :], in0=S["a"][:, :], scalar1=-1.0, scalar2=prm[:, 8:9], op0=ALU.add, op1=ALU.mult))
            dv(lambda e: e.scalar_tensor_tensor(out=S["k2"][:, :], in0=S["d"][:, :], scalar=1.0, in1=S["xk"][:, :], op0=ALU.add, op1=ALU.mult))
            dv(lambda e: e.tensor_tensor(out=S["d"][:, :], in0=S["xr"][:, :], in1=S["k2"][:, :], op=ALU.mult), eng="pool")
            dv(lambda e: e.tensor_scalar(out=S["d"][:, :], in0=S["d"][:, :], scalar1=prm[:, 9:10], scalar2=None, op0=ALU.mult))
            for blk in range(W // 512):
                bs = slice(blk * 512, (blk + 1) * 512)
                p, rp = psA.next()
                fw.op("pe", lambda e, p=p: e.matmul(p[:, :], lhsT=bones[:, :], rhs=S["d"][:, bs], start=True, stop=True), reads=[R, RC], writes=[rp])
                fw.op("dve", lambda e, p=p: e.tensor_tensor(out=S["bon"][:, bs], in0=p[:, :], in1=S["xv"][:, bs], op=ALU.mult), reads=[rp, R], writes=[R])
            cur = S["nlw"]
            nxt_names = ["cumA", "cumB"]
            sft = 1; pp = 0
            while sft < 128:
                nx = S[nxt_names[pp]]
                c3 = cur[:, :].rearrange("p (n t) -> p n t", t=128)
                n3 = nx[:, :].rearrange("p (n t) -> p n t", t=128)
                dv(lambda e, c3=c3, n3=n3, sft=sft: e.tensor_tensor(out=n3[:, :, sft:], in0=c3[:, :, sft:], in1=c3[:, :, :128 - sft], op=ALU.add))
                dv(lambda e, c3=c3, n3=n3, sft=sft: e.tensor_copy(out=n3[:, :, :sft], in_=c3[:, :, :sft]), eng="pool")
                cur = nx
                pp ^= 1
                sft *= 2
            cum = cur
            fw.op("act", lambda e: e.activation(out=S["en"][:, :], in_=cum[:, :], func=AF.Exp, scale=-1.0), reads=[R], writes=[R])
            dv(lambda e: e.tensor_tensor(out=S["rt"][:, :], in0=S["xr"][:, :], in1=S["en"][:, :], op=ALU.mult))
            dv(lambda e: e.tensor_copy(out=PLs[:, :], in_=S["en"][:, :].rearrange("p (n t) -> p n t", t=128)[:, :, 127]))
            dv(lambda e: e.tensor_tensor(out=S["d"][:, :], in0=S["nlw"][:, :], in1=cum[:, :], op=ALU.subtract), eng="pool")
            fw.op("act", lambda e: e.activation(out=S["d"][:, :], in_=S["d"][:, :], func=AF.Exp), reads=[R], writes=[R])
            dv(lambda e: e.scalar_tensor_tensor(out=S["at"][:, :], in0=S["kk"][:, :], scalar=-1.0, in1=S["d"][:, :], op0=ALU.mult, op1=ALU.mult))
            fw.op("act", lambda e: e.activation(out=S["tmp"][:, :], in_=cum[:, :], func=AF.Exp), reads=[R], writes=[R])
            dv(lambda e: e.tensor_tensor(out=S["bt"][:, :], in0=S["kk"][:, :], in1=S["a"][:, :], op=ALU.mult), eng="pool")
            dv(lambda e: e.tensor_tensor(out=S["bt"][:, :], in0=S["bt"][:, :], in1=S["tmp"][:, :], op=ALU.mult))
            dv(lambda e: e.tensor_tensor(out=S["kt"][:, :], in0=S["k2"][:, :], in1=S["tmp"][:, :], op=ALU.mult), eng="pool")
            for ci in range(W // 128):
                cs = slice(ci * 128, (ci + 1) * 128)
                bh, rbh = bh_p.next(); kh, rkh = kh_p.next()
                fw.op("dve", lambda e, bh=bh: e.tensor_scalar(out=bh[:, :], in0=S["bt"][:, cs], scalar1=PLs[:, ci:ci + 1], scalar2=None, op0=ALU.mult), reads=[R], writes=[rbh])
                fw.op("pool", lambda e, kh=kh: e.tensor_scalar(out=kh[:, :], in0=S["kt"][:, cs], scalar1=PLs[:, ci:ci + 1], scalar2=None, op0=ALU.mult), reads=[R], writes=[rkh])
                px, rpx = psX.next()
                fw.op("pe", lambda e, px=px: e.matmul(px[:, 0:128], lhsT=S["xv"][:, cs], rhs=ident[:, :], start=True, stop=True), reads=[R, RC], writes=[rpx])
                fw.op("pe", lambda e, px=px, bh=bh: e.matmul(px[:, 128:256], lhsT=bh[:, :], rhs=ident[:, :], start=True, stop=True), reads=[rbh, RC], writes=[rpx])
                fw.op("pe", lambda e, px=px, kh=kh: e.matmul(px[:, 256:384], lhsT=kh[:, :], rhs=ident[:, :], start=True, stop=True), reads=[rkh, RC], writes=[rpx])
                tok, rtok = tok_p.next()
                fw.op("act", lambda e, px=px, tok=tok: e.copy(out=tok[:, :, :].rearrange("p a c -> p (a c)"), in_=px[:, 0:384]), reads=[rpx], writes=[rtok])
                ynb, rynb = ynb_p.next()
                for h in range(2):
                    hs = slice(h * 64, (h + 1) * 64)
                    pa, rpa = psA.next()
                    for bi, (lh, rh) in enumerate((("bt", "at"), ("kt", "at"), ("bt", "rt"), ("kt", "rt"))):
                        fw.op("pe", lambda e, pa=pa, bi=bi, lh=lh, rh=rh: e.matmul(pa[:, bi * 128:(bi + 1) * 128], lhsT=S[lh][hs, cs], rhs=S[rh][hs, cs], start=True, stop=True),
                              reads=[R], writes=[rpa])
                    AT, rAT = AT_p.next()
                    fw.op("dve", lambda e, pa=pa, AT=AT: e.tensor_tensor(out=AT[:, :, :].rearrange("p a c -> p (a c)"), in0=pa[:, :], in1=mask4[:, :], op=ALU.mult),
                          reads=[rpa, RC], writes=[rAT])
                    pn, rpn = psN.next()
                    fw.op("pe", lambda e, pn=pn: e.matmul(pn[:, 0:128], lhsT=S["at"][hs, cs], rhs=S["bt"][hs, cs], start=True, stop=True), reads=[R], writes=[rpn])
                    NN, rNN = NN_p.next()
                    fw.op("dve", lambda e, pn=pn, NN=NN: e.tensor_tensor(out=NN[:, 0, :], in0=pn[:, 0:128], in1=maskL[:, :], op=ALU.mult), reads=[rpn, RC], writes=[rNN])
                    fw.op("pool", lambda e, NN=NN, AT=AT: e.tensor_copy(out=NN[:, 1, :], in_=AT[:, 0, :]), reads=[rAT], writes=[rNN])
                    Tt, rTt = Tt_p.next()
                    fw.op("pool", lambda e, Tt=Tt, AT=AT: e.tensor_tensor(out=Tt[:, :], in0=AT[:, 0, :], in1=ident[:, :], op=ALU.add), reads=[rAT, RC], writes=[rTt])
                    for lvl in range(6):
                        pn, rpn = psN.next()
                        fw.op("pe", lambda e, pn=pn, NN=NN: e.matmul(pn[:, 0:128], lhsT=NN[:, 1, :], rhs=NN[:, 0, :], start=True, stop=True), reads=[rNN], writes=[rpn])
                        fw.op("pe", lambda e, pn=pn, NN=NN: e.matmul(pn[:, 128:256], lhsT=NN[:, 0, :], rhs=NN[:, 1, :], start=True, stop=True), reads=[rNN], writes=[rpn])
                        NN2, rNN2 = NN_p.next()
                        fw.op("act", lambda e, pn=pn, NN2=NN2: e.copy(out=NN2[:, :, :].rearrange("p a c -> p (a c)"), in_=pn[:, 0:256]), reads=[rpn], writes=[rNN2])
                        NN, rNN = NN2, rNN2
                        pt, rpt = psX.next()
                        fw.op("pe", lambda e, pt=pt, NN=NN, Tt=Tt: e.matmul(pt[:, 0:128], lhsT=NN[:, 0, :], rhs=Tt[:, :], start=True, stop=True), reads=[rNN, rTt], writes=[rpt])
                        Tt2, rTt2 = Tt_p.next()
                        fw.op("dve", lambda e, pt=pt, Tt=Tt, Tt2=Tt2: e.tensor_tensor(out=Tt2[:, :], in0=Tt[:, :], in1=pt[:, 0:128], op=ALU.add), reads=[rpt, rTt], writes=[rTt2])
                        Tt, rTt = Tt2, rTt2
                    pc, rpc = psC.next()
                    fw.op("pe", lambda e, pc=pc: e.matmul(pc[:, 0:64], lhsT=S["at"][hs, cs], rhs=ST[hs, :], start=True, stop=False), reads=[R, r_ST], writes=[rpc])
                    fw.op("pe", lambda e, pc=pc, AT=AT, tok=tok: e.matmul(pc[:, 0:64], lhsT=AT[:, 1, :], rhs=tok[:, 0, hs], start=False, stop=True), reads=[rAT, rtok], writes=[rpc])
                    x0, rx0 = x0_p.next()
                    fw.op("act", lambda e, pc=pc, x0=x0: e.copy(out=x0[:, :], in_=pc[:, 0:64]), reads=[rpc], writes=[rx0])
                    pc2, rpc2 = psC.next()
                    fw.op("pe", lambda e, pc2=pc2, Tt=Tt, x0=x0: e.matmul(pc2[:, 0:64], lhsT=Tt[:, :], rhs=x0[:, :], start=True, stop=True), reads=[rTt, rx0], writes=[rpc2])
                    uu, ruu = u_p.next()
                    fw.op("act", lambda e, pc2=pc2, uu=uu: e.copy(out=uu[:, :], in_=pc2[:, 0:64]), reads=[rpc2], writes=[ruu])
                    py, rpy = psC.next()
                    fw.op("pe", lambda e, py=py: e.matmul(py[:, 64:128], lhsT=S["rt"][hs, cs], rhs=ST[hs, :], start=True, stop=False), reads=[R, r_ST], writes=[rpy])
                    fw.op("pe", lambda e, py=py, AT=AT, uu=uu: e.matmul(py[:, 64:128], lhsT=AT[:, 2, :], rhs=uu[:, :], start=False, stop=False), reads=[rAT, ruu], writes=[rpy])
                    fw.op("pe", lambda e, py=py, AT=AT, tok=tok: e.matmul(py[:, 64:128], lhsT=AT[:, 3, :], rhs=tok[:, 0, hs], start=False, stop=True), reads=[rAT, rtok], writes=[rpy])
                    ps_, rps_ = psC.next()
                    fw.op("pe", lambda e, ps_=ps_, tok=tok, uu=uu: e.matmul(ps_[:, 128:192], lhsT=tok[:, 1, :], rhs=uu[:, :], start=True, stop=False), reads=[rtok, ruu], writes=[rps_])
                    fw.op("pe", lambda e, ps_=ps_, tok=tok: e.matmul(ps_[:, 128:192], lhsT=tok[:, 2, :], rhs=tok[:, 0, hs], start=False, stop=True), reads=[rtok], writes=[rps_])
                    fw.op("dve", lambda e, ps_=ps_: e.scalar_tensor_tensor(out=ST[hs, :], in0=ST[hs, :], scalar=PLs[hs, ci:ci + 1], in1=ps_[hs, 128:192], op0=ALU.mult, op1=ALU.add),
                          reads=[rps_, r_ST, R], writes=[r_ST])
                    yb, ryb = y_p.next(); sm, rsm = st_p.next()
                    fw.op("act", lambda e, py=py, yb=yb, sm=sm: e.activation(out=yb[:, :], in_=py[:, 64:128], func=AF.Identity, accum_out=sm[:, 0:1]), reads=[rpy], writes=[ryb, rsm])
                    fw.op("dve", lambda e, sm=sm: e.tensor_scalar(out=sm[:, 1:2], in0=sm[:, 0:1], scalar1=-1.0 / 64, scalar2=None, op0=ALU.mult), reads=[rsm], writes=[rsm])
                    yc, ryc = yc_p.next()
                    fw.op("dve", lambda e, yb=yb, yc=yc, sm=sm: e.tensor_scalar(out=yc[:, :], in0=yb[:, :], scalar1=sm[:, 1:2], scalar2=None, op0=ALU.add), reads=[ryb, rsm], writes=[ryc])
                    fw.op("act", lambda e, yc=yc, yb=yb, sm=sm: e.activation(out=yb[:, :], in_=yc[:, :], func=AF.Square, accum_out=sm[:, 2:3]), reads=[ryc, ryb], writes=[ryb, rsm])
                    fw.op("act", lambda e, sm=sm: e.activation(out=sm[:, 3:4], in_=sm[:, 2:3], func=AF.Ln, scale=1.0 / 64, bias=cst[:, 2:3]), reads=[rsm, RC], writes=[rsm])
                    fw.op("act", lambda e, sm=sm: e.activation(out=sm[:, 4:5], in_=sm[:, 3:4], func=AF.Exp, scale=-0.5), reads=[rsm], writes=[rsm])
                    fw.op("dve", lambda e, yc=yc, ynb=ynb, sm=sm: e.tensor_scalar(out=ynb[:, hs], in0=yc[:, :], scalar1=sm[:, 4:5], scalar2=None, op0=ALU.mult), reads=[ryc, rsm], writes=[rynb])
                px, rpx = psX.next()
                fw.op("pe", lambda e, px=px, ynb=ynb: e.matmul(px[:, 0:128], lhsT=ynb[:, :], rhs=ident[:, :], start=True, stop=True), reads=[rynb, RC], writes=[rpx])
                fw.op("act", lambda e, px=px: e.copy(out=S["ynT"][:, cs], in_=px[:, 0:128]), reads=[rpx, R], writes=[R])
            dv(lambda e: e.tensor_scalar(out=S["ynT"][:, :], in0=S["ynT"][:, :], scalar1=prm[:, 10:11], scalar2=prm[:, 11:12], op0=ALU.mult, op1=ALU.add))
            dv(lambda e: e.tensor_tensor(out=S["ynT"][:, :], in0=S["ynT"][:, :], in1=S["bon"][:, :], op=ALU.add), eng="pool")
            fw.op("dve", lambda e: e.tensor_tensor(out=ob[:, :], in0=S["ynT"][:, :], in1=S["g"][:, :], op=ALU.mult), reads=[R, r_ob], writes=[r_ob])
            fw.dma("sp", oT[:, t0:t0 + W], ob[:, :], reads=[r_ob], is_output=True)
        fw.finish()
    return nc


def run_rw(first, ins_list):
    nc = _get("rw%d" % int(first), lambda: build_rw(first))
    c = rw_consts()
    in_maps = [dict(c, **ins_list[i]) for i in range(NCORE)]
    res = run_bass_kernel_spmd(nc, in_maps, core_ids=list(range(NCORE)))
    return res.results


def build_p2(first):
    nc = _newnc()
    _PFX[0] = "sb_"; build_sb(nc)
    _PFX[0] = "ret_"; build_ret(nc)
    _PFX[0] = "s5_"; build_s5(nc)
    _PFX[0] = "rw_"; build_rw(first, nc)
    _PFX[0] = ""
    return nc


def build_p3p1():
    nc = _newnc()
    ext = {}
    _PFX[0] = "p3_"; build_p3(nc, ext)
    _PFX[0] = "p1_"; build_p1(nc, {"xT": ext["_outT"]})
    _PFX[0] = ""
    return nc


def _swap_idx(off):
    j = np.arange(256)
    return off + (j // 64) * 64 + ((j % 64) + 32) % 64


def p1_inputs(inp, l, pfx=""):
    w = inp["w_in_first"] if l == 0 else inp["w_in_rest"][l - 1]
    if w.shape[1] < N_IN:
        w = np.concatenate([w, np.zeros((D, N_IN - w.shape[1]), np.float32)], axis=1)
    wf = np.concatenate([w, w[:, _swap_idx(256)], w[:, _swap_idx(512)]], axis=1)
    wt = np.concatenate([w[:, 1792:2048], w[:, 768:1024], w[:, 512:768], w[:, _swap_idx(512)]], axis=1)
    return {pfx + "w_in": np.ascontiguousarray(wf), pfx + "w_tok": np.ascontiguousarray(wt),
            pfx + "gain": gain_layout(inp["norm_mix_pre"][l])}


def p3_inputs(inp, l, pfx=""):
    gl = np.ascontiguousarray(np.concatenate([gain_layout(inp["norm_mix_post"][l]), gain_layout(inp["norm_ffn_pre"][l]),
                                              gain_layout(inp["norm_ffn_post"][l])], axis=1))
    return {pfx + "glu1": np.ascontiguousarray(inp["s5_glu_w1"][l]), pfx + "glu2": np.ascontiguousarray(inp["s5_glu_w2"][l]),
            pfx + "w_out": np.ascontiguousarray(inp["w_out"][l]), pfx + "w_up": np.ascontiguousarray(inp["w_up"][l]),
            pfx + "w_down": np.ascontiguousarray(inp["w_down"][l]), pfx + "gains": gl}


_CONST_CACHE = {}


def p2_inputs(inp, l, c, PT, PK, vf):
    s = c % 2
    cs = slice(128 * s, 128 * s + 128)
    A = np.ascontiguousarray
    m = {}
    if "sb" not in _CONST_CACHE:
        _CONST_CACHE["sb"] = sb_consts()
        _CONST_CACHE["rw"] = rw_consts()
        _CONST_CACHE["ret0"] = ret_consts(0)
        _CONST_CACHE["ret1"] = ret_consts(1)
    for k, v in _CONST_CACHE["sb"].items():
        m["sb_" + k] = v
    m["sb_qT"] = A(PT[1280:1536][cs]); m["sb_kT"] = A(PT[1536:1792][cs]); m["sb_vtok"] = A(PK[:, 0:256][:, cs])
    for k, v in _CONST_CACHE["ret%d" % s].items():
        m["ret_" + k] = v
    m["ret_qT"] = A(PT[256:512][cs]); m["ret_qsT"] = A(PT[3104:3360][cs]); m["ret_kT"] = A(PT[512:768][cs])
    m["ret_ksT"] = A(PT[3360:3616][cs]); m["ret_gT"] = A(PT[1024:1280][cs])
    m["ret_vtok"] = A(PK[:, 256:512][:, cs]); m["ret_ktok"] = A(PK[:, 512:768][:, cs]); m["ret_kstok"] = A(PK[:, 768:1024][:, cs])
    lay = s5_host_layout(inp["s5_a_re"][l], inp["s5_a_im"][l], inp["s5_log_dt"][l], inp["s5_b_re"][l], inp["s5_b_im"][l],
                         inp["s5_c_re"][l], inp["s5_c_im"][l], inp["s5_d"][l], s)
    for k, v in lay.items():
        m["s5_" + k] = A(v)
    m["s5_uT"] = A(PT[0:256][cs])
    for k, v in _CONST_CACHE["rw"].items():
        m["rw_" + k] = v
    for k, v in rw_host_params(inp, l, s).items():
        m["rw_" + k] = A(v)
    m["rw_rT"] = A(PT[2048:2304][cs]); m["rw_kT"] = A(PT[2304:2560][cs]); m["rw_vT"] = A(PT[2560:2816][cs])
    m["rw_wdT"] = A(PT[2816:2880]); m["rw_adT"] = A(PT[2880:2944]); m["rw_gdT"] = A(PT[2944:3072])
    if l > 0:
        m["rw_vrT"] = A(PT[3072:3104]); m["rw_vfT"] = vf
    return m


def kernel(**inp):
    inp = {k: np.asarray(v) for k, v in inp.items()}
    x = inp["x"]
    cores = list(range(NCORE))
    xT = [np.ascontiguousarray(x[c // 2, (c % 2) * TOK:(c % 2 + 1) * TOK, :].T) for c in cores]
    nc1 = _get("p1", build_p1)
    com = p1_inputs(inp, 0)
    res = run_bass_kernel_spmd(nc1, [dict(com, xT=xT[c]) for c in cores], core_ids=cores).results
    projT = [r["projT"] for r in res]
    projK = [r["projK"] for r in res]
    vfirst = [None] * NCORE
    out = None
    for l in range(DEPTH):
        nc2 = _get("p2_%d" % int(l == 0), lambda: build_p2(l == 0))
        in_maps = []
        for b in range(BATCH):
            PT = np.concatenate([projT[2 * b], projT[2 * b + 1]], axis=1)
            PK = np.concatenate([projK[2 * b], projK[2 * b + 1]], axis=0)
            for s in range(2):
                in_maps.append(p2_inputs(inp, l, 2 * b + s, PT, PK, vfirst[2 * b + s]))
        res = run_bass_kernel_spmd(nc2, in_maps, core_ids=cores).results
        if l == 0:
            vfirst = [np.ascontiguousarray(r["rw_vfo"]) for r in res]
        mixT = []
        for c in cores:
            b, h = c // 2, c % 2
            ts_ = slice(h * TOK, (h + 1) * TOK)
            parts = []
            for key in ("s5_yT", "ret_oT", "rw_oT", "sb_oT"):
                parts.append(res[2 * b][key][:, ts_])
                parts.append(res[2 * b + 1][key][:, ts_])
            mixT.append(np.ascontiguousarray(np.concatenate(parts, axis=0)))
        if l < DEPTH - 1:
            nc3 = _get("p3p1", build_p3p1)
            com = dict(p3_inputs(inp, l, "p3_"))
            com.update(p1_inputs(inp, l + 1, "p1_"))
            res = run_bass_kernel_spmd(nc3, [dict(com, p3_xT=xT[c], p3_mixT=mixT[c]) for c in cores], core_ids=cores).results
            xT = [r["p3_outT"] for r in res]
            projT = [r["p1_projT"] for r in res]
            projK = [r["p1_projK"] for r in res]
        else:
            nc3 = _get("p3", build_p3)
            com = p3_inputs(inp, l)
            res = run_bass_kernel_spmd(nc3, [dict(com, xT=xT[c], mixT=mixT[c]) for c in cores], core_ids=cores).results
            out = [r["outT"] for r in res]
    y = np.empty((BATCH, SEQ, D), np.float32)
    for c in cores:
        y[c // 2, (c % 2) * TOK:(c % 2 + 1) * TOK, :] = np.asarray(out[c]).T
    return y
```
